# Optimizing a Trainium2 kernel written in Bass

```python
import math
import jax
import jax.numpy as jnp
from jax import lax
import numpy as np

D_MODEL = 1024
BATCH = 2
SEQ = 8192
DEPTH = 2

CTX_LEN = 256
GRID_W = 64
CONV_W = 3
EPS = 1e-6
F32 = jnp.float32

HG_HEADS = 4
HG_DK = 64
HG_DV = 64
HG_WIDTH = HG_HEADS * HG_DV
HG_CHUNK = 32

HY_WIDTH = 256
HY_ORDER = 2
HY_EMB_BANDS = 16
HY_EMB_DIM = 1 + 2 * HY_EMB_BANDS
HY_FILTER_HIDDEN = 64
HY_DECAY_TARGET = 1e-2
HY_FAST_DECAY_PCT = 0.3
HY_SLOW_DECAY_PCT = 1.5
HY_MIN_DECAY = math.log(HY_DECAY_TARGET) / HY_SLOW_DECAY_PCT
HY_MAX_DECAY = math.log(HY_DECAY_TARGET) / HY_FAST_DECAY_PCT

SSD_HEADS = 8
SSD_HEAD_DIM = 64
SSD_WIDTH = SSD_HEADS * SSD_HEAD_DIM
SSD_GROUPS = 2
SSD_STATE = 128
SSD_XBC = SSD_WIDTH + 2 * SSD_GROUPS * SSD_STATE
SSD_CHUNK = 64

MIX_WIDTH = HG_WIDTH + HY_WIDTH + SSD_WIDTH
HG_COLS = 5 * HG_WIDTH
HY_COLS = (HY_ORDER + 1) * HY_WIDTH
SSD_COLS = SSD_WIDTH + SSD_XBC + 2 * SSD_HEADS
IN_COLS = HG_COLS + HY_COLS + SSD_COLS

D_FF = 2816

kernel_name = "parallel_hybrid_flow_backbone"


def rms_norm(x, g):
    xf = x.astype(F32)
    y = xf * lax.rsqrt(jnp.mean(xf * xf, axis=-1, keepdims=True) + EPS)
    return (y * g.astype(F32)).astype(x.dtype)


def group_rms_norm(x, g, groups):
    shp = x.shape
    xf = x.astype(F32).reshape(shp[:-1] + (groups, shp[-1] // groups))
    y = xf * lax.rsqrt(jnp.mean(xf * xf, axis=-1, keepdims=True) + EPS)
    return y.reshape(shp) * g.astype(F32)


def modulate(h, shift, scale):
    return h * (1.0 + scale) + shift


def heads(t, n):
    return t.reshape(t.shape[:-1] + (n, t.shape[-1] // n))


def to_chunks(t, size):
    return t.reshape((t.shape[0], t.shape[1] // size, size) + t.shape[2:])


def dwconv1d(u, w, b):
    out = lax.conv_general_dilated(u, w[:, None, :], window_strides=(1,), padding="SAME",
                                   dimension_numbers=("NWC", "WIO", "NWC"),
                                   feature_group_count=u.shape[-1])
    return out + b


def dwconv2d(u, w, b):
    out = lax.conv_general_dilated(u, w[:, :, None, :], window_strides=(1, 1), padding="SAME",
                                   dimension_numbers=("NHWC", "HWIO", "NHWC"),
                                   feature_group_count=u.shape[-1])
    return out + b


def chunk_scan(decay, update, s0):
    def step(s, inp):
        d, u = inp
        return d * s + u, s
    s_fin, s_prev = lax.scan(step, s0, (jnp.moveaxis(decay, 1, 0), jnp.moveaxis(update, 1, 0)))
    return jnp.moveaxis(s_prev, 0, 1), s_fin


def gla_direction(q, f_logit, v, lb, s0, reverse, with_output):
    if reverse:
        q, f_logit, v = jnp.flip(q, 1), jnp.flip(f_logit, 1), jnp.flip(v, 1)
    logf = jnp.logaddexp(jnp.log(lb), jnp.log1p(-lb) + jax.nn.log_sigmoid(f_logit.astype(F32)))
    k = 1.0 - jnp.exp(logf)
    b = jnp.cumsum(to_chunks(logf, HG_CHUNK), axis=2)
    kc = to_chunks(k, HG_CHUNK)
    vc = to_chunks(v.astype(F32), HG_CHUNK)
    b_end = b[:, :, -1:]
    upd = jnp.einsum("bcshk,bcshv->bchkv", kc * jnp.exp(b_end - b), vc)
    s_prev, s_fin = chunk_scan(jnp.exp(b_end[:, :, 0])[..., None], upd, s0)
    if not with_output:
        return None, s_fin
    qc = to_chunks(q.astype(F32), HG_CHUNK)
    b_mid = b[:, :, HG_CHUNK // 2 - 1:HG_CHUNK // 2]
    scores = jnp.einsum("bcthk,bcshk->bchts", qc * jnp.exp(b - b_mid), kc * jnp.exp(b_mid - b))
    lower_tri = jnp.tril(jnp.ones((HG_CHUNK, HG_CHUNK), dtype=bool))
    scores = jnp.where(lower_tri, scores, 0.0)
    o = (jnp.einsum("bchts,bcshv->bcthv", scores, vc)
         + jnp.einsum("bcthk,bchkv->bcthv", qc * jnp.exp(b), s_prev))
    o = o.reshape(v.shape)
    if reverse:
        o = jnp.flip(o, 1)
    return o, s_fin


def ssd_direction(x, dt, a, bm, cm, s0, reverse, with_output):
    if reverse:
        x, dt, bm, cm = jnp.flip(x, 1), jnp.flip(dt, 1), jnp.flip(bm, 1), jnp.flip(cm, 1)
    bsz, seq_len, n_heads, head_dim = x.shape
    groups = bm.shape[2]
    hpg = n_heads // groups
    xc = to_chunks(x.astype(F32).reshape(bsz, seq_len, groups, hpg, head_dim), SSD_CHUNK)
    dtc = to_chunks(dt.astype(F32).reshape(bsz, seq_len, groups, hpg), SSD_CHUNK)
    bc = to_chunks(bm.astype(F32), SSD_CHUNK)
    cc = to_chunks(cm.astype(F32), SSD_CHUNK)
    a_cs = jnp.cumsum(dtc * a.reshape(groups, hpg), axis=2)
    xdt = xc * dtc[..., None]
    a_end = a_cs[:, :, -1:]
    upd = jnp.einsum("bcsgn,bcsgh,bcsghp->bcghpn", bc, jnp.exp(a_end - a_cs), xdt)
    s_prev, s_fin = chunk_scan(jnp.exp(a_end[:, :, 0])[..., None, None], upd, s0)
    if not with_output:
        return None, s_fin
    lower_tri = jnp.tril(jnp.ones((SSD_CHUNK, SSD_CHUNK), dtype=bool))
    seg = a_cs[:, :, :, None] - a_cs[:, :, None]
    decay = jnp.exp(jnp.where(lower_tri[:, :, None, None], seg, -jnp.inf))
    cb = jnp.einsum("bctgn,bcsgn->bctsg", cc, bc)
    y = jnp.einsum("bctsgh,bcsghp->bctghp", cb[..., None] * decay, xdt)
    y = y + jnp.einsum("bctgn,bcghpn,bctgh->bctghp", cc, s_prev, jnp.exp(a_cs))
    y = y.reshape(bsz, seq_len, n_heads, head_dim)
    if reverse:
        y = jnp.flip(y, 1)
    return y, s_fin


def hgrn2_branch(u, lb, norm_g, s0_f, s0_b, with_output):
    q, f_fwd, f_bwd, v, g = jnp.split(u, 5, axis=-1)
    q = heads(q, HG_HEADS) * (HG_DK ** -0.5)
    v = heads(v, HG_HEADS)
    o_f, s_f = gla_direction(q, heads(f_fwd, HG_HEADS), v, lb, s0_f, False, with_output)
    o_b, s_b = gla_direction(q, heads(f_bwd, HG_HEADS), v, lb, s0_b, True, with_output)
    if not with_output:
        return None, s_f, s_b
    o = (o_f + o_b).reshape(u.shape[:-1] + (HG_WIDTH,))
    o = group_rms_norm(o, norm_g, HG_HEADS) * jax.nn.silu(g.astype(F32))
    return o.astype(u.dtype), s_f, s_b


def hyena_filter_spectrum(seq_len, w1, b1, freq1, w2, b2, freq2, w3):
    t = jnp.linspace(0.0, 1.0, seq_len, dtype=F32)[:, None]
    w = 2.0 * math.pi * jnp.arange(seq_len, dtype=F32)[:, None] / seq_len
    bands = jnp.linspace(1e-4, HY_EMB_BANDS - 1, HY_EMB_BANDS, dtype=F32)[None]
    z = jnp.concatenate([t, jnp.cos(bands * w), -jnp.sin(bands * w)], axis=-1)
    h = jnp.sin(freq1.astype(F32) * (z @ w1.astype(F32) + b1.astype(F32)))
    h = jnp.sin(freq2.astype(F32) * (h @ w2.astype(F32) + b2.astype(F32)))
    h = (h @ w3.astype(F32)).reshape(seq_len, 2, HY_ORDER, HY_WIDTH)
    deltas = jnp.abs(jnp.linspace(HY_MIN_DECAY, HY_MAX_DECAY, HY_WIDTH, dtype=F32))
    h = h * jnp.exp(-t[:, :, None, None] * deltas)
    buf = jnp.concatenate([h[:, 0], jnp.zeros((1, HY_ORDER, HY_WIDTH), F32), h[:0:-1, 1]], axis=0)
    buf = buf / jnp.sum(jnp.abs(buf), axis=0, keepdims=True)
    return jnp.fft.rfft(buf, axis=0)


def fft_long_conv(u, spec, bias):
    seq_len = u.shape[1]
    uf = u.astype(F32)
    y = jnp.fft.irfft(jnp.fft.rfft(uf, n=2 * seq_len, axis=1) * spec, n=2 * seq_len, axis=1)[:, :seq_len]
    return y + uf * bias.astype(F32)


def hyena_branch(u, p):
    zc = dwconv1d(u, p["hy_conv_w"], p["hy_conv_b"])
    v, x1, x2 = jnp.split(zc, 3, axis=-1)
    spec = hyena_filter_spectrum(u.shape[1], p["hy_w1"], p["hy_b1"], p["hy_freq1"],
                                 p["hy_w2"], p["hy_b2"], p["hy_freq2"], p["hy_w3"])
    y = v
    for n, gate in enumerate((x1, x2)):
        y = gate.astype(F32) * fft_long_conv(y, spec[:, n], p["hy_bias"][n])
    return y.astype(u.dtype)


def ssd_branch(u, p, s0_f, s0_b, with_output):
    z, xbc, dt_raw = jnp.split(u, [SSD_WIDTH, SSD_WIDTH + SSD_XBC], axis=-1)
    xbc = jax.nn.silu(dwconv1d(xbc, p["ssd_conv_w"], p["ssd_conv_b"]))
    xs, bm, cm = jnp.split(xbc, [SSD_WIDTH, SSD_WIDTH + SSD_GROUPS * SSD_STATE], axis=-1)
    xs = heads(xs, SSD_HEADS)
    bm = heads(bm, SSD_GROUPS)
    cm = heads(cm, SSD_GROUPS)
    dt = jax.nn.softplus(heads(dt_raw.astype(F32), 2) + p["ssd_dt_bias"].astype(F32))
    a = -jnp.exp(p["ssd_a_log"].astype(F32))
    y_f, s_f = ssd_direction(xs, dt[..., 0, :], a[0], bm, cm, s0_f, False, with_output)
    y_b, s_b = ssd_direction(xs, dt[..., 1, :], a[1], bm, cm, s0_b, True, with_output)
    if not with_output:
        return None, s_f, s_b
    y = y_f + y_b + p["ssd_d"].astype(F32)[:, None] * xs.astype(F32)
    y = y.reshape(u.shape[:-1] + (SSD_WIDTH,)) * jax.nn.silu(z.astype(F32))
    return group_rms_norm(y, p["ssd_norm_g"], SSD_GROUPS).astype(u.dtype), s_f, s_b


def token_mixers(hx, hc, p, lb, ctx_out):
    bsz = hx.shape[0]
    ux = hx @ p["w_in"]
    uc = hc @ p["w_in"]
    ux_hg, ux_hy, ux_ssd = jnp.split(ux, [HG_COLS, HG_COLS + HY_COLS], axis=-1)
    uc_hg, uc_hy, uc_ssd = jnp.split(uc, [HG_COLS, HG_COLS + HY_COLS], axis=-1)
    zero_hg = jnp.zeros((bsz, HG_HEADS, HG_DK, HG_DV), F32)
    o_hg_c, s_hf, s_hb = hgrn2_branch(uc_hg, lb, p["hg_norm_g"], zero_hg, zero_hg, ctx_out)
    o_hg_x, _, _ = hgrn2_branch(ux_hg, lb, p["hg_norm_g"], s_hf, s_hb, True)
    zero_ssd = jnp.zeros((bsz, SSD_GROUPS, SSD_HEADS // SSD_GROUPS, SSD_HEAD_DIM, SSD_STATE), F32)
    o_ssd_c, s_sf, s_sb = ssd_branch(uc_ssd, p, zero_ssd, zero_ssd, ctx_out)
    o_ssd_x, _, _ = ssd_branch(ux_ssd, p, s_sf, s_sb, True)
    o_hy_x = hyena_branch(ux_hy, p)
    yx = jnp.concatenate([o_hg_x, o_hy_x, o_ssd_x], axis=-1) @ p["w_out"]
    if not ctx_out:
        return yx, None
    o_hy_c = hyena_branch(uc_hy, p)
    yc = jnp.concatenate([o_hg_c, o_hy_c, o_ssd_c], axis=-1) @ p["w_out"]
    return yx, yc


def conv_ffn(h, p, rows):
    bsz, seq_len, _ = h.shape
    a = h @ p["ffn_w_gate"]
    a = dwconv2d(a.reshape(bsz, rows, seq_len // rows, D_FF), p["ffn_conv_w"], p["ffn_conv_b"])
    a = a.reshape(bsz, seq_len, D_FF)
    return (jax.nn.silu(a) * (h @ p["ffn_w_up"])) @ p["ffn_w_down"]


def setup_inputs(seed: int = 0) -> dict:
    key = jax.random.key(seed)
    ks = iter(jax.random.split(key, 48))

    def nrm(shape, scale):
        return scale * jax.random.normal(next(ks), shape, F32)

    dt0 = jnp.exp(jax.random.uniform(next(ks), (DEPTH, 2, SSD_HEADS), F32, math.log(1e-3), math.log(1e-1)))
    return {
        "x": nrm((BATCH, SEQ, D_MODEL), 1.0),
        "c": nrm((BATCH, D_MODEL), 1.0),
        "ctx": nrm((BATCH, CTX_LEN, D_MODEL), 1.0),
        "c_ctx": nrm((D_MODEL,), 1.0),
        "w_ada": nrm((DEPTH, D_MODEL, 6 * D_MODEL), 0.5 * D_MODEL ** -0.5),
        "b_ada": nrm((DEPTH, 6 * D_MODEL), 0.02),
        "norm1_g": 1.0 + nrm((DEPTH, D_MODEL), 0.05),
        "norm2_g": 1.0 + nrm((DEPTH, D_MODEL), 0.05),
        "w_in": nrm((DEPTH, D_MODEL, IN_COLS), D_MODEL ** -0.5),
        "w_out": nrm((DEPTH, MIX_WIDTH, D_MODEL), MIX_WIDTH ** -0.5),
        "hg_lb_logits": nrm((DEPTH, HG_HEADS * HG_DK), 1.0),
        "hg_norm_g": 1.0 + nrm((DEPTH, HG_WIDTH), 0.05),
        "hy_conv_w": nrm((DEPTH, CONV_W, HY_COLS), CONV_W ** -0.5),
        "hy_conv_b": nrm((DEPTH, HY_COLS), 0.02),
        "hy_w1": nrm((DEPTH, HY_EMB_DIM, HY_FILTER_HIDDEN), HY_EMB_DIM ** -0.5),
        "hy_b1": nrm((DEPTH, HY_FILTER_HIDDEN), 0.1),
        "hy_freq1": 1.0 + nrm((DEPTH, HY_FILTER_HIDDEN), 0.05),
        "hy_w2": nrm((DEPTH, HY_FILTER_HIDDEN, HY_FILTER_HIDDEN), HY_FILTER_HIDDEN ** -0.5),
        "hy_b2": nrm((DEPTH, HY_FILTER_HIDDEN), 0.1),
        "hy_freq2": 1.0 + nrm((DEPTH, HY_FILTER_HIDDEN), 0.05),
        "hy_w3": nrm((DEPTH, HY_FILTER_HIDDEN, 2 * HY_ORDER * HY_WIDTH), HY_FILTER_HIDDEN ** -0.5),
        "hy_bias": nrm((DEPTH, HY_ORDER, HY_WIDTH), 1.0),
        "ssd_conv_w": nrm((DEPTH, CONV_W, SSD_XBC), CONV_W ** -0.5),
        "ssd_conv_b": nrm((DEPTH, SSD_XBC), 0.02),
        "ssd_dt_bias": dt0 + jnp.log(-jnp.expm1(-dt0)),
        "ssd_a_log": jnp.log(jax.random.uniform(next(ks), (DEPTH, 2, SSD_HEADS), F32, 1.0, 16.0)),
        "ssd_d": 1.0 + nrm((DEPTH, SSD_HEADS), 0.1),
        "ssd_norm_g": 1.0 + nrm((DEPTH, SSD_WIDTH), 0.05),
        "ffn_w_gate": nrm((DEPTH, D_MODEL, D_FF), D_MODEL ** -0.5),
        "ffn_w_up": nrm((DEPTH, D_MODEL, D_FF), D_MODEL ** -0.5),
        "ffn_conv_w": nrm((DEPTH, 3, 3, D_FF), 1.0 / 3.0),
        "ffn_conv_b": nrm((DEPTH, D_FF), 0.02),
        "ffn_w_down": nrm((DEPTH, D_FF, D_MODEL), D_FF ** -0.5),
        "final_norm_g": 1.0 + nrm((D_MODEL,), 0.05),
    }


def reference(x, c, ctx, c_ctx, w_ada, b_ada, norm1_g, norm2_g, w_in, w_out,
              hg_lb_logits, hg_norm_g, hy_conv_w, hy_conv_b, hy_w1, hy_b1, hy_freq1,
              hy_w2, hy_b2, hy_freq2, hy_w3, hy_bias, ssd_conv_w, ssd_conv_b,
              ssd_dt_bias, ssd_a_log, ssd_d, ssd_norm_g, ffn_w_gate, ffn_w_up,
              ffn_conv_w, ffn_conv_b, ffn_w_down, final_norm_g):
    rows = x.shape[1] // GRID_W
    lb_all = jnp.cumsum(jax.nn.softmax(hg_lb_logits.astype(F32), axis=0), axis=0)
    lb_all = lb_all - lb_all[0]
    for l in range(DEPTH):
        last = l == DEPTH - 1
        p = {
            "w_in": w_in[l], "w_out": w_out[l], "hg_norm_g": hg_norm_g[l],
            "hy_conv_w": hy_conv_w[l], "hy_conv_b": hy_conv_b[l],
            "hy_w1": hy_w1[l], "hy_b1": hy_b1[l], "hy_freq1": hy_freq1[l],
            "hy_w2": hy_w2[l], "hy_b2": hy_b2[l], "hy_freq2": hy_freq2[l],
            "hy_w3": hy_w3[l], "hy_bias": hy_bias[l],
            "ssd_conv_w": ssd_conv_w[l], "ssd_conv_b": ssd_conv_b[l],
            "ssd_dt_bias": ssd_dt_bias[l], "ssd_a_log": ssd_a_log[l],
            "ssd_d": ssd_d[l], "ssd_norm_g": ssd_norm_g[l],
            "ffn_w_gate": ffn_w_gate[l], "ffn_w_up": ffn_w_up[l],
            "ffn_conv_w": ffn_conv_w[l], "ffn_conv_b": ffn_conv_b[l],
            "ffn_w_down": ffn_w_down[l],
        }
        mx = [m[:, None, :] for m in jnp.split(jax.nn.silu(c) @ w_ada[l] + b_ada[l], 6, axis=-1)]
        mc = jnp.split(jax.nn.silu(c_ctx) @ w_ada[l] + b_ada[l], 6, axis=-1)
        hx = modulate(rms_norm(x, norm1_g[l]), mx[0], mx[1])
        hc = modulate(rms_norm(ctx, norm1_g[l]), mc[0], mc[1])
        yx, yc = token_mixers(hx, hc, p, lb_all[l].reshape(HG_HEADS, HG_DK), not last)
        x = x + mx[2] * yx
        x = x + mx[5] * conv_ffn(modulate(rms_norm(x, norm2_g[l]), mx[3], mx[4]), p, rows)
        if not last:
            ctx = ctx + mc[2] * yc
            ctx = ctx + mc[5] * conv_ffn(modulate(rms_norm(ctx, norm2_g[l]), mc[3], mc[4]), p, 1)
    return rms_norm(x, final_norm_g)
```

```python
import contextlib
import numpy as np
import concourse.bass as bass
import concourse.mybir as mybir
from concourse.bass_utils import run_bass_kernel_spmd

F32 = mybir.dt.float32
BF16 = mybir.dt.bfloat16
ACT = mybir.ActivationFunctionType
ALU = mybir.AluOpType
AX = mybir.AxisListType
AP = bass.AP

ENG = ['pe', 'act', 'dve', 'pool', 'sp']
NDMA = 8


class T:
    def __init__(self, ap, key):
        self.ap = ap
        self.key = key

    def __getitem__(self, idx):
        return T(self.ap[idx], self.key)

    def k(self, sub):
        return T(self.ap, (self.key[0], sub))

    def re(self, s, **kw):
        return T(self.ap.rearrange(s, **kw), self.key)

    def view(self, off, dims, np_=None):
        a = self.ap
        pd = list(a.ap[0])
        if np_ is not None:
            pd = [pd[0], np_]
        return T(AP(a.tensor, a.offset + off, [pd] + [list(d) for d in dims]), self.key)

    def bc(self, dt):
        return T(self.ap.bitcast(dt), self.key)


class Prog:
    def __init__(self, nc, same_engine_sync=True):
        self.nc = nc
        self.es = contextlib.ExitStack()
        self.engs = {'pe': nc.tensor, 'act': nc.scalar, 'dve': nc.vector, 'pool': nc.gpsimd, 'sp': nc.sync}
        self.q = {e: [] for e in ENG}
        self.sems = {}
        self.cnt = {}
        for e in ENG[:4]:
            self.sems[e] = self.es.enter_context(nc.semaphore('s_' + e))
            self.cnt[e] = 0
        for i in range(NDMA):
            self.sems['d%d' % i] = self.es.enter_context(nc.semaphore('s_d%d' % i))
            self.cnt['d%d' % i] = 0
        self.seen = {e: {} for e in ENG}
        self.tab = {}
        self.ndma = 0
        self.ses = same_engine_sync
        self.nuniq = 0

    def sb(self, name, shape, dt=F32):
        t = self.es.enter_context(self.nc.sbuf_tensor(name, list(shape), dt))
        return T(t[:] if len(shape) == 2 else t[tuple(slice(None) for _ in shape)], (name, 0))

    def ps(self, name, shape, dt=F32):
        t = self.es.enter_context(self.nc.psum_tensor(name, list(shape), dt))
        return T(t[:], (name, 0))

    def dram(self, name, shape, dt=F32, kind="Internal"):
        t = self.nc.dram_tensor(name, list(shape), dt, kind=kind)
        return T(t.ap(), (name, 0))

    def _recs(self, key):
        name, sub = key
        d = self.tab.setdefault(name, {})
        if sub == 0:
            return list(d.values())
        out = []
        if 0 in d:
            out.append(d[0])
        if sub in d:
            out.append(d[sub])
        return out

    def _deps(self, eng, reads, writes):
        need = {}

        def add(src):
            if src is None:
                return
            s, v = src
            if (not self.ses or eng == 'pe') and s == eng:
                return
            if need.get(s, 0) < v:
                need[s] = v
        for r in reads:
            for rec in self._recs(r.key):
                add(rec[0])
        for w in writes:
            for rec in self._recs(w.key):
                add(rec[0])
                for rd in rec[1]:
                    add(rd)
        out = []
        for s, v in need.items():
            if self.seen[eng].get(s, 0) < v:
                self.seen[eng][s] = v
                out.append((s, v))
        return out

    def _record(self, src, reads, writes):
        for r in reads:
            name, sub = r.key
            d = self.tab.setdefault(name, {})
            rec = d.setdefault(sub, [None, []])
            rec[1].append(src)
            if len(rec[1]) > 64:
                mx = {}
                for s, v in rec[1]:
                    mx[s] = max(mx.get(s, 0), v)
                rec[1] = list(mx.items())
        for w in writes:
            name, sub = w.key
            d = self.tab.setdefault(name, {})
            if sub == 0:
                rec = d.setdefault(0, [None, []])
                rec[0] = src
                rec[1] = []
            else:
                rec = d.setdefault(sub, [None, []])
                rec[0] = src
                rec[1] = []

    def op(self, eng, fn, reads, writes):
        waits = self._deps(eng, reads, writes)
        self.cnt[eng] += 1
        v = self.cnt[eng]
        sem = self.sems[eng]
        self.q[eng].append((waits, fn, sem, 1))
        self._record((eng, v), reads, writes)

    def dma(self, out, in_, eng='sp', **kw):
        k = 'd%d' % (self.ndma % NDMA)
        self.ndma += 1
        waits = self._deps(eng, [in_], [out])
        prev = self.cnt[k]
        if prev > 0 and self.seen[eng].get(k, 0) < prev:
            self.seen[eng][k] = prev
            waits.append((k, prev))
        self.cnt[k] += 16
        v = self.cnt[k]
        o, i = out.ap, in_.ap
        self.q[eng].append((waits, lambda e: e.dma_start(out=o, in_=i, **kw), self.sems[k], 16))
        self._record((k, v), [in_], [out])

    def wait_all(self, eng='sp'):
        waits = []
        for s, v in self.cnt.items():
            if v > 0 and self.seen[eng].get(s, 0) < v:
                self.seen[eng][s] = v
                waits.append((s, v))
        self.q[eng].append((waits, None, None, 0))

    def emit(self):
        nc = self.nc
        with nc.Block() as block:
            def run(e, name):
                for waits, fn, sem, inc in self.q[name]:
                    for s, v in waits:
                        e.wait_ge(self.sems[s], v)
                    if fn is not None:
                        fn(e).then_inc(sem, inc)

            @block.tensor
            def _(e):
                run(e, 'pe')

            @block.scalar
            def _(e):
                run(e, 'act')

            @block.vector
            def _(e):
                run(e, 'dve')

            @block.gpsimd
            def _(e):
                run(e, 'pool')

            @block.sync
            def _(e):
                run(e, 'sp')
        self.es.close()

    def mm(self, out, lhsT, rhs, start=True, stop=True, extra_reads=()):
        o, l, r = out.ap, lhsT.ap, rhs.ap
        rd = [lhsT, rhs] + list(extra_reads)
        if not start:
            rd.append(out)
        self.op('pe', lambda e: e.matmul(o, l, r, start=start, stop=stop), rd, [out])

    def tr(self, out, in_, ident):
        o, i, d = out.ap, in_.ap, ident.ap
        self.op('pe', lambda e: e.transpose(o, i, d), [in_, ident], [out])

    def act(self, out, in_, func, bias=None, scale=None, accum=None, eng='act'):
        o, i = out.ap, in_.ap
        kw = {}
        rd = [in_]
        wr = [out]
        if bias is not None:
            if isinstance(bias, T):
                kw['bias'] = bias.ap
                rd.append(bias)
            else:
                kw['bias'] = bias
        if scale is not None:
            if isinstance(scale, T):
                kw['scale'] = scale.ap
                rd.append(scale)
            else:
                kw['scale'] = scale
        if accum is not None:
            kw['accum_out'] = accum.ap
            wr.append(accum)
        self.op('act', lambda e: e.activation(o, i, func, **kw), rd, wr)

    def tt(self, out, a, b, op, eng='dve'):
        o, x, y = out.ap, a.ap, b.ap
        self.op(eng, lambda e: e.tensor_tensor(o, x, y, op), [a, b], [out])

    def ts(self, out, a, s1, op0, s2=None, op1=None, eng='dve', accum=None):
        o, x = out.ap, a.ap
        rd = [a]
        wr = [out]
        v1 = s1.ap if isinstance(s1, T) else s1
        v2 = s2.ap if isinstance(s2, T) else s2
        if isinstance(s1, T):
            rd.append(s1)
        if isinstance(s2, T):
            rd.append(s2)
        kw = {}
        if op1 is not None:
            kw['op1'] = op1
        if accum is not None:
            kw['accum_out'] = accum.ap
            wr.append(accum)
        self.op(eng, lambda e: e.tensor_scalar(o, x, v1, v2, op0, **kw), rd, wr)

    def stt(self, out, a, s, b, op0, op1, accum=None):
        o, x, y = out.ap, a.ap, b.ap
        rd = [a, b]
        wr = [out]
        sv = s.ap if isinstance(s, T) else s
        if isinstance(s, T):
            rd.append(s)
        kw = {}
        if accum is not None:
            kw['accum_out'] = accum.ap
            wr.append(accum)
        self.op('dve', lambda e: e.scalar_tensor_tensor(o, x, sv, y, op0, op1, **kw), rd, wr)

    def scan(self, out, d0, d1, init, op0=ALU.mult, op1=ALU.add):
        o, x, y = out.ap, d0.ap, d1.ap
        rd = [d0, d1]
        iv = init.ap if isinstance(init, T) else init
        if isinstance(init, T):
            rd.append(init)
        self.op('dve', lambda e: e.tensor_tensor_scan(o, x, y, iv, op0, op1), rd, [out])

    def copy(self, out, in_, eng='dve'):
        o, i = out.ap, in_.ap
        if eng == 'act':
            self.op('act', lambda e: e.copy(o, i), [in_], [out])
        else:
            self.op(eng, lambda e: e.tensor_copy(o, i), [in_], [out])

    def memset(self, out, val, eng='dve'):
        o = out.ap
        self.op(eng, lambda e: e.memset(o, val), [], [out])

    def recip(self, out, in_):
        o, i = out.ap, in_.ap
        self.op('dve', lambda e: e.reciprocal(o, i), [in_], [out])

    def reduce(self, out, in_, op=ALU.add, axis=AX.X):
        o, i = out.ap, in_.ap
        self.op('dve', lambda e: e.tensor_reduce(o, i, axis, op), [in_], [out])


EPS = 1e-6


def din(nc, name, shape, dt=F32):
    return T(nc.dram_tensor(name, list(shape), dt, kind="ExternalInput").ap(), (name, 0))


def dout(nc, name, shape, dt=F32):
    return T(nc.dram_tensor(name, list(shape), dt, kind="ExternalOutput").ap(), (name, 0))


def tiles(n, sz):
    return [(s, min(sz, n - s)) for s in range(0, n, sz)]


class PsRot:
    def __init__(self, p, n=6, pref='psr'):
        self.t = [p.ps('%s%d' % (pref, i), [128, 512]) for i in range(n)]
        self.i = 0

    def get(self):
        t = self.t[self.i % len(self.t)]
        self.i += 1
        return t


def run(nc, in_maps, ncores=8):
    res = run_bass_kernel_spmd(nc, in_maps, core_ids=list(range(ncores)))
    return res.results


def build_M():
    nc = bass.Bass("TRN2", target_bir_lowering=False)
    p = Prog(nc)
    cT = din(nc, "cT", [128, 24])
    w = din(nc, "w", [1024, 1536])
    b = din(nc, "b", [128, 12])
    o = dout(nc, "o", [128, 36])
    cs = p.sb('cs', [128, 24]); sc = p.sb('sc', [128, 24])
    ws = p.sb('ws', [128, 8, 1536]); bs = p.sb('bs', [128, 12]); os_ = p.sb('os', [128, 36])
    ps = p.ps('ps', [128, 512])
    p.dma(cs, cT); p.dma(bs, b)
    wv = w.re("(k p) n -> p k n", p=128)
    for k in range(8):
        p.dma(ws[:, k, :].k(('k', k)), wv[:, k, :])
    p.act(sc, cs, ACT.Silu)
    for j in range(12):
        for k in range(8):
            p.mm(ps[:, 3 * j:3 * j + 3], ws[:, k, 128 * j:128 * (j + 1)].k(('k', k)), sc[:, 3 * k:3 * k + 3],
                 start=(k == 0), stop=(k == 7))
    p.tt(os_.view(0, [[3, 12], [1, 3]]), ps[:, 0:36].view(0, [[3, 12], [1, 3]]), bs.view(0, [[1, 12], [0, 3]]), ALU.add)
    p.dma(o, os_)
    p.wait_all('sp')
    p.emit()
    return nc


def stage_M(c, c_ctx, w_ada, b_ada):
    nc = build_M()
    cc = np.concatenate([c, c_ctx[None]], 0)
    cT = np.ascontiguousarray(cc.T.reshape(8, 128, 3).transpose(1, 0, 2).reshape(128, 24))
    wf = np.concatenate([w_ada[0], w_ada[1]], 1)
    bf = np.concatenate([b_ada[0], b_ada[1]], 0)
    maps = []
    for i in range(8):
        maps.append({"cT": cT, "w": np.ascontiguousarray(wf[:, 1536 * i:1536 * (i + 1)]),
                     "b": np.ascontiguousarray(bf[1536 * i:1536 * (i + 1)].reshape(12, 128).T)})
    res = run(nc, maps)
    outs = [r["o"].reshape(128, 12, 3).transpose(1, 0, 2).reshape(1536, 3) for r in res]
    full = np.concatenate(outs, 0)
    return full.reshape(2, 6144, 3)


def fm_norm_mod(p, psr, xs, KC, segs, ones, gm, sh, hout, rstd, sqb, tmpb, nfeat):
    for (s0, sn, mi) in segs:
        for (t0, tn) in [(s0 + a, b) for a, b in tiles(sn, 512)]:
            ps = psr.get()
            for k in range(KC):
                sq = sqb[k % 2]
                p.act(sq[:, :tn], xs[:, k, t0:t0 + tn], ACT.Square)
                p.mm(ps[:, :tn], ones, sq[:, :tn], start=(k == 0), stop=(k == KC - 1))
            p.act(rstd[:, :tn], ps[:, :tn], ACT.Sqrt, scale=1.0 / nfeat, bias=p.epsc)
            p.recip(rstd[:, :tn], rstd[:, :tn])
            for k in range(KC):
                tb = tmpb[k % 2]
                p.tt(tb[:, :tn], xs[:, k, t0:t0 + tn], rstd[:, :tn], ALU.mult)
                p.ts(hout[:, k, t0:t0 + tn], tb[:, :tn], gm[:, mi, k:k + 1], ALU.mult, sh[:, mi, k:k + 1], ALU.add,
                     eng='pool' if k % 2 else 'dve')


def load_consts(p, nc):
    p.epsc = p.sb('epsc', [128, 1])
    p.memset(p.epsc, EPS)
    ones = p.sb('ones', [128, 128])
    p.memset(ones, 1.0)
    return ones


def fm_linear(p, psr, w, K, Ncols, hT, Ntok, evac, wst, wbf, colblk=512):
    KC = K // 128
    wv = w.re("(k p) n -> p k n", p=128)
    for bi, (c0, cn) in enumerate(tiles(Ncols, colblk)):
        wf = wst[bi % 2]
        wb = wbf[bi % 2]
        p.dma(wf[:, :, :cn], wv[:, :, c0:c0 + cn])
        p.copy(wb[:, :, :cn], wf[:, :, :cn], eng='pool')
        for (m0, mn) in tiles(cn, 128):
            for (t0, tn) in tiles(Ntok, 512):
                ps = psr.get()
                for k in range(KC):
                    p.mm(ps[:mn, :tn], wb[:, k, m0:m0 + mn], hT[:, k, t0:t0 + tn], start=(k == 0), stop=(k == KC - 1))
                evac(c0 + m0, mn, t0, tn, ps)


def build_A(NL, NCX, ncols=3600):
    N = NL + NCX
    nc = bass.Bass("TRN2", target_bir_lowering=False)
    p = Prog(nc)
    xT = din(nc, "xT", [1024, N])
    mod = din(nc, "mod", [128, 4, 8])
    g = din(nc, "g", [128, 8])
    w = din(nc, "w", [1024, ncols])
    uT = dout(nc, "uT", [ncols, N])
    ones = load_consts(p, nc)
    psr = PsRot(p, 7)
    xs = p.sb('xs', [128, 8, N])
    hT = p.sb('hT', [128, 8, N], BF16)
    mods = p.sb('mods', [128, 4, 8]); gs = p.sb('gs', [128, 8])
    gm = p.sb('gm', [128, 2, 8]); sh = p.sb('sh', [128, 2, 8])
    rstd = p.sb('rstd', [128, 512])
    sqb = [p.sb('sq%d' % i, [128, 512]) for i in range(2)]
    tmpb = [p.sb('tb%d' % i, [128, 512]) for i in range(2)]
    wst = [p.sb('wst%d' % i, [128, 8, 512]) for i in range(2)]
    wbf = [p.sb('wbf%d' % i, [128, 8, 512], BF16) for i in range(2)]
    ost = [p.sb('ost%d' % i, [128, N]) for i in range(2)]
    p.dma(mods, mod); p.dma(gs, g)
    xv = xT.re("(k p) n -> p k n", p=128)
    for k in range(8):
        p.dma(xs[:, k, :], xv[:, k, :])
    for i in range(2):
        p.ts(gm[:, i, :], mods[:, 2 * i + 1, :], 1.0, ALU.add)
        p.tt(gm[:, i, :], gm[:, i, :], gs, ALU.mult)
        p.copy(sh[:, i, :], mods[:, 2 * i, :])
    segs = [(0, NL, 0)] + ([(NL, NCX, 1)] if NCX else [])
    fm_norm_mod(p, psr, xs, 8, segs, ones, gm, sh, hT, rstd, sqb, tmpb, 1024.0)
    state = {'n': 0}

    def evac(m0, mn, t0, tn, ps):
        ob = ost[(m0 // 128) % 2]
        if state['n'] % 2:
            p.copy(ob[:mn, t0:t0 + tn], ps[:mn, :tn], eng='act')
        else:
            p.copy(ob[:mn, t0:t0 + tn], ps[:mn, :tn], eng='dve')
        state['n'] += 1
        if t0 + tn == N:
            p.dma(uT[m0:m0 + mn, :], ob[:mn, :])
    fm_linear(p, psr, w, 1024, ncols, hT, N, evac, wst, wbf)
    p.wait_all('sp')
    p.emit()
    return nc


def modvec(v):
    return np.ascontiguousarray(v.reshape(8, 128).T)


def stage_A(xT_lat, xT_ctx, modl, g1, w_in):
    NCX = 64 if xT_ctx is not None else 0
    nc = build_A(2048, NCX)
    maps = []
    for i in range(8):
        b, q = i // 4, i % 4
        xin = xT_lat[b][:, 2048 * q:2048 * (q + 1)]
        if NCX:
            xin = np.concatenate([xin, xT_ctx[b][:, 64 * q:64 * (q + 1)]], 1)
        mod = np.stack([modvec(modl[0:1024, b]), modvec(modl[1024:2048, b]),
                        modvec(modl[0:1024, 2]), modvec(modl[1024:2048, 2])], 1)
        maps.append({"xT": np.ascontiguousarray(xin), "mod": np.ascontiguousarray(mod), "g": modvec(g1), "w": w_in})
    res = run(nc, maps)
    uL = np.zeros((2, 3600, 8192), np.float32)
    uC = np.zeros((2, 3600, 256), np.float32) if NCX else None
    for i in range(8):
        b, q = i // 4, i % 4
        u = res[i]["uT"]
        uL[b][:, 2048 * q:2048 * (q + 1)] = u[:, :2048]
        if NCX:
            uC[b][:, 64 * q:64 * (q + 1)] = u[:, 2048:]
    return uL, uC


def build_B1(T_, nitem=2, TT=1408):
    NB = T_ // 128
    NCH = T_ // 32
    nc = bass.Bass("TRN2", target_bir_lowering=False)
    p = Prog(nc)
    qTs = [din(nc, "qT%d" % i, [64, T_]) for i in range(nitem)]
    fTs = [din(nc, "fT%d" % i, [64, T_]) for i in range(nitem)]
    vts = [din(nc, "vt%d" % i, [128, NB, 64]) for i in range(nitem)]
    vhs = [din(nc, "vh%d" % i, [64, 2 * NB, 64]) for i in range(nitem)]
    lbs = [din(nc, "lb%d" % i, [64, 3]) for i in range(nitem)]
    maskd = din(nc, "mask", [128, 128])
    identd = din(nc, "ident", [64, 64])
    oTs = [dout(nc, "oT%d" % i, [64, T_]) for i in range(nitem)]
    psr = PsRot(p, 4)
    pst = p.ps('pst', [128, 1024], BF16)
    psu2 = [p.ps('psu%d' % i, [128, 512]) for i in range(2)]
    A = p.sb('A', [64, TT]); Kk = p.sb('K', [64, TT]); B = p.sb('B', [64, TT]); Q = p.sb('Q', [64, TT])
    E1 = p.sb('E1', [64, TT]); E2 = p.sb('E2', [64, TT])
    msk = p.sb('msk', [64, TT])
    kd = p.sb('kd', [64, T_], BF16); qd = p.sb('qd', [64, T_], BF16); qb = p.sb('qb', [64, T_], BF16)
    kdec = p.sb('kdec', [64, T_], BF16)
    dch = p.sb('dch', [64, NCH])
    big = p.sb('big', [128, T_])
    vf = big[:, 0:NB * 64].re("p (j c) -> p j c", c=64); vb = p.sb('vb', [128, NB, 64], BF16)
    vhf = big[0:64, 0:2 * NB * 64].re("p (j c) -> p j c", c=64); vhb = p.sb('vhb', [64, 2 * NB, 64], BF16)
    Sall = p.sb('Sall', [64, NCH + 1, 64], BF16)
    S32 = [p.sb('S32_%d' % i, [64, 64]) for i in range(2)]
    kdt = [p.sb('kdt%d' % i, [64, 2, 64], BF16) for i in range(2)]
    scT = [p.sb('scT%d' % i, [128, 128], BF16) for i in range(2)]
    oT = big[0:64, :]
    mask = p.sb('mask_s', [128, 128]); identf = p.sb('identf', [64, 64]); ident = p.sb('ident_s', [64, 64], BF16)
    lb = p.sb('lb_s', [64, 1]); oml = p.sb('oml', [64, 1]); lb3 = p.sb('lb3', [64, 3])
    p.dma(mask, maskd); p.dma(identf, identd)
    p.copy(ident, identf)
    p.memset(msk, 1.0)
    p.memset(msk.view(0, [[32, TT // 32]]), 0.0)
    SC = 64 ** -0.5
    ncht = TT // 32
    for it in range(nitem):
        p.dma(lb3, lbs[it])
        p.tt(lb, lb3[:, 1:2], lb3[:, 0:1], ALU.subtract)
        p.act(lb, lb, ACT.Sigmoid)
        p.tt(lb, lb, lb3[:, 2:3], ALU.mult)
        p.ts(oml, lb, -1.0, ALU.mult, 1.0, ALU.add)
        p.dma(vf, vts[it])
        p.copy(vb, vf, eng='pool')
        p.dma(vhf, vhs[it])
        p.copy(vhb, vhf, eng='pool')
        for ti, (t0, tn) in enumerate(tiles(T_, TT)):
            p.dma(A, fTs[it][:, t0:t0 + tn]); p.dma(Q, qTs[it][:, t0:t0 + tn])
            p.act(Kk, A, ACT.Sigmoid, scale=-1.0)
            p.ts(Kk, Kk, oml, ALU.mult)
            p.act(E1, A, ACT.Sigmoid)
            p.ts(E1, E1, oml, ALU.mult, lb, ALU.add)
            p.act(A, E1, ACT.Ln)
            p.scan(B, msk, A, 0.0)
            bend = B.view(31, [[32, ncht]])
            c0 = t0 // 32
            p.act(dch[:, c0:c0 + ncht], bend, ACT.Exp)
            p.tt(E1.view(0, [[32, ncht], [1, 32]]), B.view(0, [[32, ncht], [1, 32]]), B.view(15, [[32, ncht], [0, 32]]), ALU.subtract)
            p.act(E2, E1, ACT.Exp)
            p.stt(qd[:, t0:t0 + tn], E2, SC, Q, ALU.mult, ALU.mult)
            p.act(E2, E1, ACT.Exp, scale=-1.0)
            p.tt(kd[:, t0:t0 + tn], Kk, E2, ALU.mult)
            p.tt(E1.view(0, [[32, ncht], [1, 32]]), B.view(0, [[32, ncht], [1, 32]]), B.view(31, [[32, ncht], [0, 32]]), ALU.subtract)
            p.act(E2, E1, ACT.Exp, scale=-1.0)
            p.tt(kdec[:, t0:t0 + tn], Kk, E2, ALU.mult)
            p.act(E2, B, ACT.Exp)
            p.stt(qb[:, t0:t0 + tn], E2, SC, Q, ALU.mult, ALU.mult)
        p.memset(Sall[:, 0, :], 0.0)
        p.memset(S32[0], 0.0)
        sidx = 0
        for j in range(NB):
            kt = kdt[j % 2]
            for hb in range(2):
                p.tr(pst[0:64, 64 * hb:64 * (hb + 1)], kdec[:, 128 * j + 64 * hb:128 * j + 64 * (hb + 1)], ident)
            p.copy(kt, pst[0:64, 0:128].view(0, [[64, 2], [1, 64]]))
            for c in range(4):
                hb, c2 = c // 2, c % 2
                p.mm(psu2[c2][0:64, 64 * hb:64 * (hb + 1)], kt[32 * c2:32 * (c2 + 1), hb, :], vhb[32 * c2:32 * (c2 + 1), 2 * j + hb, :])
            for c in range(4):
                ch = 4 * j + c
                sc_, sn_ = S32[sidx % 2], S32[(sidx + 1) % 2]
                p.stt(sn_, sc_, dch[:, ch:ch + 1], psu2[c % 2][0:64, 64 * (c // 2):64 * (c // 2 + 1)], ALU.mult, ALU.add)
                p.copy(Sall[:, ch + 1, :], sn_, eng='act')
                sidx += 1
        for j in range(NB):
            ps_s = psr.get()
            blk = slice(128 * j, 128 * (j + 1))
            p.mm(ps_s[:, 0:128], kd[:, blk], qd[:, blk])
            st = scT[j % 2]
            p.tt(st, ps_s[:, 0:128], mask, ALU.mult)
            ps_o = psr.get()
            p.mm(ps_o[0:64, 0:128], vb[:, j, :], st, start=True, stop=False)
            for c in range(4):
                ch = 4 * j + c
                p.mm(ps_o[0:64, 32 * c:32 * (c + 1)], Sall[:, ch, :], qb[:, 128 * j + 32 * c:128 * j + 32 * (c + 1)],
                     start=False, stop=(c == 3))
            p.copy(oT[:, blk], ps_o[0:64, 0:128], eng='act')
        p.dma(oTs[it], oT)
    p.wait_all('sp')
    p.emit()
    return nc


def hg_mask():
    s = np.arange(128)[:, None]; t = np.arange(128)[None, :]
    return ((s // 32 == t // 32) & (s <= t)).astype(np.float32)


def seq_cat(uc, ul, rev):
    if rev:
        return np.concatenate([uc[:, ::-1], ul[:, ::-1]], 1)
    return np.concatenate([uc, ul], 1)


def seq_split(o, rev):
    oc, ol = o[:, :256], o[:, 256:]
    if rev:
        return oc[:, ::-1], ol[:, ::-1]
    return oc, ol


def stage_B1(uL, uC, lbl, flag):
    T_ = 8448
    nc = build_B1(T_)
    items = [(b, h, d) for b in range(2) for h in range(4) for d in range(2)]
    maps = []
    for i in range(8):
        m = {"mask": hg_mask(), "ident": np.eye(64, dtype=np.float32)}
        for s in range(2):
            b, h, d = items[2 * i + s]
            cq = slice(64 * h, 64 * h + 64)
            cf = slice(256 + 256 * d + 64 * h, 256 + 256 * d + 64 * h + 64)
            cv = slice(768 + 64 * h, 768 + 64 * h + 64)
            m["qT%d" % s] = np.ascontiguousarray(seq_cat(uC[b][cq], uL[b][cq], d))
            m["fT%d" % s] = np.ascontiguousarray(seq_cat(uC[b][cf], uL[b][cf], d))
            v = seq_cat(uC[b][cv], uL[b][cv], d)
            m["vt%d" % s] = np.ascontiguousarray(v.T.reshape(T_ // 128, 128, 64).transpose(1, 0, 2))
            m["vh%d" % s] = np.ascontiguousarray(v.T.reshape(T_ // 64, 64, 64).transpose(1, 0, 2))
            m["lb%d" % s] = np.ascontiguousarray(np.stack([lbl[0, 64 * h:64 * h + 64], lbl[1, 64 * h:64 * h + 64],
                                                           np.full(64, flag, np.float32)], 1).astype(np.float32))
        maps.append(m)
    res = run(nc, maps)
    oL = [np.zeros((2, 256, 8192), np.float32) for _ in range(2)]
    oC = [np.zeros((2, 256, 256), np.float32) for _ in range(2)]
    for i in range(8):
        for s in range(2):
            b, h, d = items[2 * i + s]
            oc, ol = seq_split(res[i]["oT%d" % s], d)
            oL[d][b][64 * h:64 * h + 64] = ol
            oC[d][b][64 * h:64 * h + 64] = oc
    return oL, oC


def build_B2(T_, segs, TT=1408):
    NB = T_ // 128
    nc = bass.Bass("TRN2", target_bir_lowering=False)
    p = Prog(nc)
    xbc = din(nc, "xbc", [512, T_])
    cw = din(nc, "cw", [128, 4, 3]); cb = din(nc, "cb", [128, 4])
    dtl = din(nc, "dtl", [2, 4, T_])
    hpar = din(nc, "hpar", [2, 4, 2])
    rc = din(nc, "rc", [2, 8])
    i2d = din(nc, "i2", [2, 2])
    maskd = din(nc, "maskb", [128, 128]); identd = din(nc, "ident", [128, 128])
    xso = dout(nc, "xso", [256, T_])
    yo = dout(nc, "yo", [4, 64, T_])
    psr = PsRot(p, 4)
    psu2 = [p.ps('psu%d' % i, [128, 512]) for i in range(2)]
    pst = p.ps('pst', [128, 512])
    U = p.sb('U', [128, T_]); O = p.sb('O', [128, T_])
    xst = p.sb('xst', [128, NB, 256], BF16); Bt = p.sb('Bt', [128, NB, 128], BF16)
    Bb = p.sb('Bb', [128, T_], BF16); Cb = p.sb('Cb', [128, T_], BF16)
    cws = p.sb('cws', [128, 4, 3]); cbs = p.sb('cbs', [128, 4])
    maskb = p.sb('maskb_s', [128, 128]); ident = p.sb('ident_s', [128, 128])
    rcs = p.sb('rcs', [2, 8]); i2 = p.sb('i2s', [2, 2]); hps = p.sb('hps', [2, 4, 2])
    zo = p.sb('zo', [2, 128]); av = p.sb('av', [2, 4])
    p.dma(cws, cw); p.dma(cbs, cb); p.dma(maskb, maskd); p.dma(ident, identd)
    p.dma(rcs, rc); p.dma(i2, i2d); p.dma(hps, hpar)
    p.memset(zo, 1.0)
    p.ts(zo, zo, rcs[:, 6:7], ALU.mult)
    p.act(av, hps.view(1, [[2, 4]]), ACT.Exp)
    p.ts(av, av, -1.0, ALU.mult)
    xv = xbc.re("(k p) n -> p k n", p=128)
    for k in range(4):
        p.dma(U, xv[:, k, :])
        p.ts(O, U, cws[:, k, 1:2], ALU.mult, cbs[:, k:k + 1], ALU.add)
        for (s0, sn) in segs:
            p.stt(O[:, s0 + 1:s0 + sn], U[:, s0:s0 + sn - 1], cws[:, k, 0:1], O[:, s0 + 1:s0 + sn], ALU.mult, ALU.add)
            p.stt(O[:, s0:s0 + sn - 1], U[:, s0 + 1:s0 + sn], cws[:, k, 2:3], O[:, s0:s0 + sn - 1], ALU.mult, ALU.add)
        p.act(O, O, ACT.Silu)
        if k < 2:
            p.dma(xso[128 * k:128 * (k + 1), :], O)
            for j in range(NB):
                p.tr(pst[:, 0:128], O[:, 128 * j:128 * (j + 1)], ident)
                p.copy(xst[:, j, 128 * k:128 * (k + 1)], pst[:, 0:128], eng='act' if j % 2 else 'dve')
        elif k == 2:
            p.copy(Bb, O, eng='pool')
            for j in range(NB):
                p.tr(pst[:, 0:128], O[:, 128 * j:128 * (j + 1)], ident)
                p.copy(Bt[:, j, :], pst[:, 0:128], eng='act' if j % 2 else 'dve')
        else:
            p.copy(Cb, O, eng='pool')
    R = {n: p.sb('r_' + n, [2, TT]) for n in ['x', 'dt', 'acs', 'e', 'LT', 'RT', 'DW', 'msk']}
    p.memset(R['msk'], 1.0)
    p.memset(R['msk'].view(0, [[64, TT // 64]]), 0.0)
    S32 = [[p.sb('S32_%d_%d' % (h, i), [128, 64]) for i in range(2)] for h in range(4)]
    Sb = [p.sb('Sb%d' % i, [128, 64], BF16) for i in range(2)]
    for h in range(4):
        p.memset(S32[h][0], 0.0)
    DWt = p.sb('DWt', [128, 2])
    ea = p.sb('ea', [128, 128]); Cdec = p.sb('Cdec', [128, 128], BF16)
    sg = p.sb('sg', [128, 128]); dec = p.sb('dec', [128, 128], BF16); MT = p.sb('MT', [128, 128], BF16)
    xdt = p.sb('xdt', [128, 64], BF16); xw = p.sb('xw', [128, 64], BF16)
    yst = [U[0:64, i * TT:(i + 1) * TT].k(('y', i)) for i in range(2)]
    ncht = TT // 64
    nblk = TT // 128
    for ti, (t0, tn) in enumerate(tiles(T_, TT)):
        for h in range(4):
            p.dma(R['x'], dtl[:, h, t0:t0 + tn])
            p.act(R['e'], R['x'], ACT.Exp, bias=hps[:, h, 0:1])
            p.act(R['dt'], R['e'], ACT.Ln, bias=1.0)
            p.ts(R['e'], R['dt'], av[:, h:h + 1], ALU.mult)
            p.scan(R['acs'], R['msk'], R['e'], 0.0)
            p.ts(R['LT'], R['acs'], rcs[:, 0:1], ALU.mult, rcs[:, 1:2], ALU.add)
            p.ts(R['RT'], R['acs'], rcs[:, 2:3], ALU.mult, rcs[:, 3:4], ALU.add)
            p.tt(R['e'].view(0, [[64, ncht], [1, 64]]), R['acs'].view(63, [[64, ncht], [0, 64]]),
                 R['acs'].view(0, [[64, ncht], [1, 64]]), ALU.subtract)
            p.act(R['e'], R['e'], ACT.Exp)
            p.ts(R['e'], R['e'], rcs[:, 4:5], ALU.mult, rcs[:, 5:6], ALU.add)
            p.tt(R['DW'], R['dt'], R['e'], ALU.mult)
            ys = yst[h % 2]
            for jb in range(nblk):
                j = t0 // 128 + jb
                lb_ = slice(128 * jb, 128 * (jb + 1))
                gb = slice(128 * j, 128 * (j + 1))
                ps1 = psr.get()
                p.mm(ps1[:, 0:2], R['DW'][:, lb_], i2)
                p.copy(DWt, ps1[:, 0:2], eng='act')
                p.ts(xdt, xst[:, j, 64 * h:64 * (h + 1)], DWt[:, 0:1], ALU.mult)
                p.ts(xw, xst[:, j, 64 * h:64 * (h + 1)], DWt[:, 1:2], ALU.mult, eng='pool')
                p.mm(ps1[:, 128:256], zo, R['RT'][:, lb_])
                p.act(ea, ps1[:, 128:256], ACT.Exp)
                p.tt(Cdec, Cb[:, gb], ea, ALU.mult)
                s0_, s1_ = S32[h][0], S32[h][1]
                for c in range(2):
                    p.mm(psu2[c][:, 0:64], Bt[64 * c:64 * (c + 1), j, :], xw[64 * c:64 * (c + 1), :])
                p.copy(Sb[0], s0_, eng='act')
                p.stt(s1_, s0_, ea[:, 63:64], psu2[0][:, 0:64], ALU.mult, ALU.add)
                p.copy(Sb[1], s1_, eng='act')
                p.stt(s0_, s1_, ea[:, 127:128], psu2[1][:, 0:64], ALU.mult, ALU.add)
                ps2 = psr.get()
                p.mm(ps2[:, 0:128], R['LT'][:, lb_], R['RT'][:, lb_])
                p.tt(sg, ps2[:, 0:128], maskb, ALU.add)
                p.act(dec, sg, ACT.Exp)
                p.mm(ps2[:, 128:256], Bb[:, gb], Cb[:, gb])
                p.tt(MT, ps2[:, 128:256], dec, ALU.mult)
                ps3 = psr.get()
                p.mm(ps3[0:64, 0:128], xdt, MT, start=True, stop=False)
                for c in range(2):
                    p.mm(ps3[0:64, 64 * c:64 * (c + 1)], Sb[c], Cdec[:, 64 * c:64 * (c + 1)], start=False, stop=(c == 1))
                p.copy(ys[:, lb_], ps3[0:64, 0:128], eng='act')
            p.dma(yo[h, :, t0:t0 + tn], ys)
    p.wait_all('sp')
    p.emit()
    return nc


def ssd_maskb():
    s = np.arange(128)[:, None]; t = np.arange(128)[None, :]
    return np.where((s // 64 == t // 64) & (s <= t), 0.0, -30000.0).astype(np.float32)


def stage_B2(uL, uC, cw, cbias, dt_bias, a_log):
    T_ = 8448
    nc = build_B2(T_, [(0, 256), (256, 8192)])
    items = [(b, g, d) for b in range(2) for g in range(2) for d in range(2)]
    base = 1280 + 768
    maps = []
    rc = np.array([[-1, 0, 0, 1, 0, 1, 0, 0], [0, 1, 1, 0, 1, 0, 1, 0]], np.float32)
    for i in range(8):
        b, g, d = items[i]
        cols = np.concatenate([np.arange(256 * g, 256 * g + 256), 512 + np.arange(128 * g, 128 * g + 128),
                               768 + np.arange(128 * g, 128 * g + 128)])
        ucols = base + 512 + cols
        x = seq_cat(uC[b][ucols], uL[b][ucols], d)
        w = cw[:, cols]
        if d:
            w = w[::-1]
        dcols = base + 512 + 1024 + 8 * d + 4 * g + np.arange(4)
        dl = seq_cat(uC[b][dcols], uL[b][dcols], d)
        hp = np.stack([dt_bias[d, 4 * g:4 * g + 4], a_log[d, 4 * g:4 * g + 4]], 1)
        maps.append({"xbc": np.ascontiguousarray(x),
                     "cw": np.ascontiguousarray(w.T.reshape(4, 128, 3).transpose(1, 0, 2)),
                     "cb": np.ascontiguousarray(cbias[cols].reshape(4, 128).T),
                     "dtl": np.ascontiguousarray(np.broadcast_to(dl[None], (2, 4, T_))),
                     "hpar": np.ascontiguousarray(np.broadcast_to(hp[None], (2, 4, 2))),
                     "rc": rc, "i2": np.eye(2, dtype=np.float32), "maskb": ssd_maskb(),
                     "ident": np.eye(128, dtype=np.float32)})
    res = run(nc, maps)
    yL = [np.zeros((2, 512, 8192), np.float32) for _ in range(2)]
    yC = [np.zeros((2, 512, 256), np.float32) for _ in range(2)]
    xsL = np.zeros((2, 512, 8192), np.float32); xsC = np.zeros((2, 512, 256), np.float32)
    for i in range(8):
        b, g, d = items[i]
        yc, yl = seq_split(res[i]["yo"].reshape(256, T_), d)
        yL[d][b][256 * g:256 * g + 256] = yl; yC[d][b][256 * g:256 * g + 256] = yc
        if d == 0:
            xc, xl = seq_split(res[i]["xso"], 0)
            xsL[b][256 * g:256 * g + 256] = xl; xsC[b][256 * g:256 * g + 256] = xc
    return yL, yC, xsL, xsC


MAGIC = 12582912.0
TWO_PI = 6.283185307179586


def build_B3(L, dbg=9):
    NBk = L // 128
    HL = 2 * L + 256
    OFF = L + 127
    NCc = 32
    nc = bass.Bass("TRN2", target_bir_lowering=False)
    p = Prog(nc)
    uhy = din(nc, "uhy", [96, 2, L]); cw = din(nc, "cw", [96, 3]); cb = din(nc, "cb", [96, 1])
    zT = din(nc, "zT", [33, 2, L]); tmat = din(nc, "tmat", [128, L]); ndel = din(nc, "ndel", [128, 1])
    w1 = din(nc, "w1", [33, 64]); w2 = din(nc, "w2", [64, 64]); w3 = din(nc, "w3", [64, 128])
    mlp = din(nc, "mlp", [64, 4])
    hbias = din(nc, "hbias", [64, 1]); nsel = din(nc, "nsel", [128, 1])
    Pmd = din(nc, "Pm", [128, 128]); Jd = din(nc, "J", [128, 128]); identd = din(nc, "ident", [128, 128])
    outd = dout(nc, "out", [128, NCc, 2, NBk])
    Hb = T(nc.dram_tensor("Hb", [64, HL], BF16, kind="Internal").ap(), ('Hb', 0))
    psr = PsRot(p, 6)
    F = p.sb('F', [128, L]); O1 = p.sb('O1', [128, L]); Fb = p.sb('Fb', [128, L], BF16)
    strips = [p.sb('strip%d' % i, [128, 2 * L], BF16) for i in range(2)]
    Vt = p.sb('Vt', [128, 2, NBk, NCc], BF16); X1 = p.sb('X1', [128, 2, NBk, NCc], BF16); X2 = p.sb('X2', [128, 2, NBk, NCc], BF16)
    Vr = p.sb('Vr', [128, 2 * NBk * NCc], BF16); Z = p.sb('Z', [128, NCc, 2, NBk], BF16)
    ost = [p.sb('ost%d' % i, [128, 2, NBk]) for i in range(2)]
    w1s = p.sb('w1s', [33, 64]); w2s = p.sb('w2s', [64, 64]); w3s = p.sb('w3s', [64, 128]); mlps = p.sb('mlps', [64, 4])
    hbs = p.sb('hbs', [64, 1]); nsels = p.sb('nsels', [128, 1]); ndels = p.sb('ndels', [128, 1])
    Pm = p.sb('Pms', [128, 128]); Jf = p.sb('Jf', [128, 128]); J = p.sb('Js', [128, 128], BF16); ident = p.sb('idents', [128, 128])
    cws = p.sb('cws', [96, 3]); cbs = p.sb('cbs', [96, 1])
    for a, b in [(w1s, w1), (w2s, w2), (w3s, w3), (mlps, mlp), (hbs, hbias), (nsels, nsel), (ndels, ndel), (Pm, Pmd), (Jf, Jd),
                 (ident, identd), (cws, cw), (cbs, cb)]:
        p.dma(a, b)
    p.copy(J, Jf)
    zt = p.sb('zt', [33, 512]); tm = p.sb('tm', [128, 512]); dk = p.sb('dk', [128, 512])
    ar = p.sb('ar', [64, 512]); kk = p.sb('kk', [64, 512]); h1 = p.sb('h1', [64, 512]); h2 = p.sb('h2', [64, 512])

    def sin_layer(hout, ps, bcol, fcol, n):
        p.ts(ar[:, :n], ps[0:64, :n], mlps[:, bcol:bcol + 1], ALU.add, mlps[:, fcol:fcol + 1], ALU.mult)
        p.ts(kk[:, :n], ar[:, :n], 1.0 / TWO_PI, ALU.mult, MAGIC, ALU.add)
        p.ts(kk[:, :n], kk[:, :n], MAGIC, ALU.subtract)
        p.stt(ar[:, :n], kk[:, :n], -TWO_PI, ar[:, :n], ALU.mult, ALU.add)
        p.ts(ar[:, :n], ar[:, :n], 3.14159, ALU.min, -3.14159, ALU.max)
        p.act(hout[:, :n], ar[:, :n], ACT.Sin)
    for (t0, tn) in tiles(L, 512):
        p.dma(tm[:, :tn], tmat[:, t0:t0 + tn])
        p.act(dk[:, :tn], tm[:, :tn], ACT.Exp, scale=ndels)
        for r in range(2):
            p.dma(zt[:, :tn], zT[:, r, t0:t0 + tn])
            ps = psr.get()
            p.mm(ps[0:64, :tn], w1s, zt[:, :tn])
            sin_layer(h1, ps, 0, 1, tn)
            ps = psr.get()
            p.mm(ps[0:64, :tn], w2s, h1[:, :tn])
            sin_layer(h2, ps, 2, 3, tn)
            ps = psr.get()
            p.mm(ps[:, :tn], w3s, h2[:, :tn])
            rs = slice(64 * r, 64 * (r + 1))
            p.tt(F[rs, t0:t0 + tn], ps[rs, :tn], dk[rs, :tn], ALU.mult)
    s1 = p.sb('s1', [128, 1]); a1 = p.sb('a1', [128, 1]); rinv = p.sb('rinv', [128, 1])
    p.act(O1, F, ACT.Abs, accum=s1)
    p.act(a1, F[:, L - 1:L], ACT.Abs)
    p.stt(s1, a1, nsels, s1, ALU.mult, ALU.add)
    ps = psr.get()
    p.mm(ps[:, 0:1], Pm, s1)
    p.recip(rinv, ps[:, 0:1])
    p.ts(Fb, F, rinv, ALU.mult)
    p.ts(Fb[0:64, 0:1], F[0:64, 0:1], rinv[0:64, :], ALU.mult, hbs, ALU.add)
    zb = O1.bc(BF16)[0:64, 0:2 * L]
    p.memset(zb, 0.0)
    p.dma(Hb[:, 0:L], zb[:, 0:L])
    p.dma(Hb[:, L:2 * L], zb[:, L:2 * L])
    p.dma(Hb[:, 2 * L:HL], zb[:, 0:256])
    p.dma(Hb[:, OFF:OFF + L], Fb[0:64, :])
    p.dma(Hb[:, 128:128 + L - 1], Fb[64:128, 0:L - 1])
    for b in (range(2) if dbg >= 2 else []):
        U1 = F[0:96, :]
        Oc = O1[0:96, :]
        p.dma(U1, uhy[:, b, :])
        p.ts(Oc, U1, cws[:, 1:2], ALU.mult, cbs, ALU.add)
        p.stt(Oc[:, 1:L], U1[:, 0:L - 1], cws[:, 0:1], Oc[:, 1:L], ALU.mult, ALU.add)
        p.stt(Oc[:, 0:L - 1], U1[:, 1:L], cws[:, 2:3], Oc[:, 0:L - 1], ALU.mult, ALU.add)
        for j in (range(NBk) if dbg >= 3 else []):
            ps = psr.get()
            p.tr(ps[:, 0:128], O1[:, 128 * j:128 * (j + 1)], ident)
            import os
            e3 = os.environ.get('B3E', 'dve,dve,dve').split(',')
            p.copy(Vt[:, b, j, :], ps[:, 0:32], eng=e3[0])
            p.copy(X1[:, b, j, :], ps[:, 32:64], eng=e3[1])
            p.copy(X2[:, b, j, :], ps[:, 64:96], eng=e3[2])
    W = 2 * NBk

    def reverse(dst, sf):
        tot = NCc * W
        for (c0, cn) in tiles(tot, 512):
            ps = psr.get()
            p.mm(ps[:, :cn], J, sf[:, c0:c0 + cn])
            p.copy(dst[:, c0:c0 + cn], ps[:, :cn], eng='act')
    reverse(Vr, Vt.re("p b j c -> p (b j c)"))
    ks = [0] + [k for k in range(-(NBk - 1), NBk) if k != 0]

    def conv_pass(o, inner, gate, to_out):
        for c in range(NCc):
            st = strips[c % 2]
            hoff = (o * 32 + c) * HL
            for hh in range(2):
                srcap = T(AP(Hb.ap.tensor, hoff + hh * L, [[1, 128], [1, L]]), Hb.key)
                p.dma(st[:, hh * L:(hh + 1) * L], srcap)
            ps = psr.get()
            for ki, k in enumerate(ks):
                m0 = max(0, k); m1 = min(NBk - 1, NBk - 1 + k); cnt = m1 - m0 + 1; j0 = m0 - k
                n0 = 128 * k + L
                if inner:
                    rhs = Vr.view(c + j0 * NCc, [[NBk * NCc, 2], [NCc, cnt]])
                else:
                    rhs = Vr.view(c * W + j0, [[NBk, 2], [1, cnt]])
                p.mm(ps[:, 0:W].view(m0, [[NBk, 2], [1, cnt]]), st[:, n0:n0 + 128], rhs,
                     start=(ki == 0), stop=(ki == len(ks) - 1))
            g = gate.re("p b j c -> p (b j c)").view(c, [[NBk * NCc, 2], [NCc, NBk]])
            psv = ps[:, 0:W].view(0, [[NBk, 2], [1, NBk]])
            if to_out:
                ob = ost[c % 2]
                p.tt(ob, psv, g, ALU.mult)
                p.dma(outd[:, c, :, :], ob)
            else:
                p.tt(Z[:, c, :, :], psv, g, ALU.mult)
    conv_pass(0, True, X1, False)
    reverse(Vr, Z.re("p c b j -> p (c b j)"))
    conv_pass(1, False, X2, True)
    p.wait_all('sp')
    p.emit()
    return nc


def hy_consts(L):
    t = np.linspace(0.0, 1.0, L, dtype=np.float32)[:, None]
    w = (2.0 * np.pi * np.arange(L, dtype=np.float32)[:, None] / L).astype(np.float32)
    bands = np.linspace(1e-4, 15, 16, dtype=np.float32)[None]
    z = np.concatenate([t, np.cos(bands * w), -np.sin(bands * w)], axis=-1).astype(np.float32)
    zT = np.stack([z.T, z[::-1].T], 1)
    import math
    mn = math.log(1e-2) / 1.5; mx = math.log(1e-2) / 0.3
    deltas = np.abs(np.linspace(mn, mx, 256, dtype=np.float32))
    return np.ascontiguousarray(zT), t[:, 0], deltas


def stage_B3(uhyT, L, P, dbg=9):
    nc = build_B3(L, dbg)
    zT, t, deltas = hy_consts(L)
    tmat = np.concatenate([np.broadcast_to(t[None], (64, L)), np.broadcast_to(t[::-1][None], (64, L))], 0).astype(np.float32)
    Pm = (np.arange(128)[:, None] % 64 == np.arange(128)[None, :] % 64).astype(np.float32)
    J = np.eye(128, dtype=np.float32)[::-1].copy()
    nsel = np.concatenate([np.zeros(64), -np.ones(64)]).astype(np.float32).reshape(128, 1)
    maps = []
    NBk = L // 128
    for i in range(8):
        ch = np.arange(32 * i, 32 * i + 32)
        rows = np.concatenate([ch, 256 + ch, 512 + ch])
        uh = np.stack([uhyT[0][rows], uhyT[1][rows]], 1)
        w3cols = np.concatenate([d * 512 + o * 256 + ch for d in range(2) for o in range(2)])
        ndel = -np.tile(deltas[ch], 4).reshape(128, 1).astype(np.float32)
        maps.append({"uhy": np.ascontiguousarray(uh), "cw": np.ascontiguousarray(P['hy_conv_w'][:, rows].T),
                     "cb": np.ascontiguousarray(P['hy_conv_b'][rows].reshape(96, 1)),
                     "zT": zT, "tmat": np.ascontiguousarray(tmat), "ndel": ndel,
                     "w1": P['hy_w1'], "w2": P['hy_w2'], "w3": np.ascontiguousarray(P['hy_w3'][:, w3cols]),
                     "mlp": np.ascontiguousarray(np.stack([P['hy_b1'], P['hy_freq1'], P['hy_b2'], P['hy_freq2']], 1)),
                     "hbias": np.ascontiguousarray(np.concatenate([P['hy_bias'][0][ch], P['hy_bias'][1][ch]]).reshape(64, 1)),
                     "nsel": nsel, "Pm": Pm, "J": J, "ident": np.eye(128, dtype=np.float32)})
    res = run(nc, maps)
    o = np.zeros((2, 256, L), np.float32)
    for i in range(8):
        r = res[i]["out"]
        o[:, 32 * i:32 * i + 32, :] = r.transpose(2, 1, 3, 0).reshape(2, 32, L)
    return o


def build_C1(NL, NCX):
    N = NL + NCX
    nc = bass.Bass("TRN2", target_bir_lowering=False)
    p = Prog(nc)
    xT = din(nc, "xT", [1024, N])
    hgo = din(nc, "hgo", [2, 256, N]); hgg = din(nc, "hgg", [256, N])
    sy = din(nc, "sy", [2, 512, N]); sxs = din(nc, "sxs", [512, N]); sz = din(nc, "sz", [512, N])
    hyo = din(nc, "hyo", [256, N])
    pv = din(nc, "pv", [128, 10])
    w = din(nc, "w", [1024, 1024])
    mod = din(nc, "mod", [128, 6, 8])
    g2 = din(nc, "g2", [128, 8])
    bones = din(nc, "bones", [128, 128])
    x1o = dout(nc, "x1o", [1024, N]); h2o = dout(nc, "h2o", [1024, N])
    ones = load_consts(p, nc)
    psr = PsRot(p, 7)
    xs = p.sb('xs', [128, 8, N]); oT = p.sb('oT', [128, 8, N], BF16)
    tb = [p.sb('t%d' % i, [128, N]) for i in range(4)]
    pvs = p.sb('pvs', [128, 10]); mods = p.sb('mods', [128, 6, 8]); g2s = p.sb('g2s', [128, 8]); bo = p.sb('bo', [128, 128])
    gm = p.sb('gm', [128, 2, 8]); sh = p.sb('sh', [128, 2, 8])
    rstd = p.sb('rstd', [128, 512]); sq = [p.sb('sq%d' % i, [128, 512]) for i in range(2)]
    wst = [p.sb('wst%d' % i, [128, 8, 512]) for i in range(2)]
    wbf = [p.sb('wbf%d' % i, [128, 8, 512], BF16) for i in range(2)]
    hst = [p.sb('hst%d' % i, [128, 512]) for i in range(2)]
    for a, b in [(pvs, pv), (mods, mod), (g2s, g2), (bo, bones)]:
        p.dma(a, b)
    xv = xT.re("(k p) n -> p k n", p=128)
    for k in range(8):
        p.dma(xs[:, k, :], xv[:, k, :])
    for i in range(2):
        p.ts(gm[:, i, :], mods[:, 3 * i + 2, :], 1.0, ALU.add)
        p.tt(gm[:, i, :], gm[:, i, :], g2s, ALU.mult)
        p.copy(sh[:, i, :], mods[:, 3 * i + 1, :])
    for k in range(2):
        rs = slice(128 * k, 128 * (k + 1))
        p.dma(tb[0], hgo[0, rs, :]); p.dma(tb[1], hgo[1, rs, :]); p.dma(tb[2], hgg[rs, :])
        p.tt(tb[0], tb[0], tb[1], ALU.add)
        p.act(tb[2], tb[2], ACT.Silu)
        for (t0, tn) in tiles(N, 512):
            ps = psr.get()
            p.act(sq[0][:, :tn], tb[0][:, t0:t0 + tn], ACT.Square)
            p.mm(ps[:, :tn], bo, sq[0][:, :tn])
            p.act(rstd[:, :tn], ps[:, :tn], ACT.Sqrt, scale=1.0 / 64, bias=p.epsc)
            p.recip(rstd[:, :tn], rstd[:, :tn])
            p.tt(tb[1][:, t0:t0 + tn], tb[0][:, t0:t0 + tn], rstd[:, :tn], ALU.mult)
        p.stt(oT[:, k, :], tb[1], pvs[:, k:k + 1], tb[2], ALU.mult, ALU.mult)
    for k in range(2):
        p.dma(tb[0], hyo[128 * k:128 * (k + 1), :])
        p.copy(oT[:, 2 + k, :], tb[0], eng='pool')
    for g in range(2):
        ys = [tb[0], tb[1]]
        for c in range(2):
            k = 2 * g + c
            rs = slice(128 * k, 128 * (k + 1))
            p.dma(ys[c], sy[0, rs, :]); p.dma(tb[2], sy[1, rs, :])
            p.tt(ys[c], ys[c], tb[2], ALU.add)
            p.dma(tb[2], sxs[rs, :])
            p.stt(ys[c], tb[2], pvs[:, 2 + k:3 + k], ys[c], ALU.mult, ALU.add)
            p.dma(tb[3], sz[rs, :])
            p.act(tb[3], tb[3], ACT.Silu)
            p.tt(ys[c], ys[c], tb[3], ALU.mult)
        for (t0, tn) in tiles(N, 512):
            ps = psr.get()
            for c in range(2):
                p.act(sq[c][:, :tn], ys[c][:, t0:t0 + tn], ACT.Square)
                p.mm(ps[:, :tn], ones, sq[c][:, :tn], start=(c == 0), stop=(c == 1))
            p.act(rstd[:, :tn], ps[:, :tn], ACT.Sqrt, scale=1.0 / 256, bias=p.epsc)
            p.recip(rstd[:, :tn], rstd[:, :tn])
            for c in range(2):
                k = 2 * g + c
                p.stt(oT[:, 4 + k, t0:t0 + tn], ys[c][:, t0:t0 + tn], pvs[:, 6 + k:7 + k], rstd[:, :tn], ALU.mult, ALU.mult)

    def evac(m0, mn, t0, tn, ps):
        m = m0 // 128
        mi = 0 if t0 < NL else 1
        p.stt(xs[:, m, t0:t0 + tn], ps[:, :tn], mods[:, 3 * mi, m:m + 1], xs[:, m, t0:t0 + tn], ALU.mult, ALU.add)
    fm_linear(p, psr, w, 1024, 1024, oT, N, evac, wst, wbf)
    x1v = x1o.re("(k p) n -> p k n", p=128)
    for k in range(8):
        p.dma(x1v[:, k, :], xs[:, k, :])
    h2v = h2o.re("(k p) n -> p k n", p=128)
    segs = [(0, NL, 0)] + ([(NL, NCX, 1)] if NCX else [])
    cnt = 0
    for (s0, sn, mi) in segs:
        for (t0, tn) in [(s0 + a, b) for a, b in tiles(sn, 512)]:
            ps = psr.get()
            for k in range(8):
                p.act(sq[k % 2][:, :tn], xs[:, k, t0:t0 + tn], ACT.Square)
                p.mm(ps[:, :tn], ones, sq[k % 2][:, :tn], start=(k == 0), stop=(k == 7))
            p.act(rstd[:, :tn], ps[:, :tn], ACT.Sqrt, scale=1.0 / 1024, bias=p.epsc)
            p.recip(rstd[:, :tn], rstd[:, :tn])
            for k in range(8):
                hb = hst[cnt % 2]
                cnt += 1
                p.tt(hb[:, :tn], xs[:, k, t0:t0 + tn], rstd[:, :tn], ALU.mult)
                p.ts(hb[:, :tn], hb[:, :tn], gm[:, mi, k:k + 1], ALU.mult, sh[:, mi, k:k + 1], ALU.add)
                p.dma(h2v[:, k, t0:t0 + tn], hb[:, :tn])
    p.wait_all('sp')
    p.emit()
    return nc


def cat_lc(lat, ctx, b, q, has_ctx):
    a = lat[b][:, 2048 * q:2048 * (q + 1)]
    if has_ctx:
        a = np.concatenate([a, ctx[b][:, 64 * q:64 * (q + 1)]], 1)
    return np.ascontiguousarray(a)


def stage_C1(xT_lat, xT_ctx, uL, uC, oL, oC, yL, yC, xsL, xsC, ohL, ohC, modl, P):
    has_ctx = xT_ctx is not None
    NCX = 64 if has_ctx else 0
    nc = build_C1(2048, NCX)
    bones = np.kron(np.eye(2, dtype=np.float32), np.ones((64, 64), np.float32))
    pv = np.concatenate([P['hg_norm_g'].reshape(2, 128).T, np.repeat(P['ssd_d'], 64).reshape(4, 128).T,
                         P['ssd_norm_g'].reshape(4, 128).T], 1).astype(np.float32)
    maps = []
    base = 1280 + 768
    for i in range(8):
        b, q = i // 4, i % 4

        def cl(lat, ctx):
            return cat_lc(lat, ctx, b, q, has_ctx)

        def mv(j, col):
            return modvec(modl[1024 * j:1024 * (j + 1), col])
        mod = np.stack([mv(2, b), mv(3, b), mv(4, b), mv(2, 2), mv(3, 2), mv(4, 2)], 1)
        m = {"xT": cl(xT_lat, xT_ctx),
             "hgo": np.stack([cl(oL[0], oC[0] if has_ctx else None), cl(oL[1], oC[1] if has_ctx else None)]),
             "hgg": cl(uL[:, 1024:1280], uC[:, 1024:1280] if has_ctx else None),
             "sy": np.stack([cl(yL[0], yC[0] if has_ctx else None), cl(yL[1], yC[1] if has_ctx else None)]),
             "sxs": cl(xsL, xsC), "sz": cl(uL[:, base:base + 512], uC[:, base:base + 512] if has_ctx else None),
             "hyo": cl(ohL, ohC), "pv": np.ascontiguousarray(pv), "w": P['w_out'],
             "mod": np.ascontiguousarray(mod), "g2": modvec(P['norm2_g']), "bones": bones}
        maps.append(m)
    res = run(nc, maps)
    x1L = np.zeros((2, 1024, 8192), np.float32); h2L = np.zeros((2, 1024, 8192), np.float32)
    x1C = np.zeros((2, 1024, 256), np.float32) if has_ctx else None
    h2C = np.zeros((2, 1024, 256), np.float32) if has_ctx else None
    for i in range(8):
        b, q = i // 4, i % 4
        x1L[b][:, 2048 * q:2048 * (q + 1)] = res[i]["x1o"][:, :2048]
        h2L[b][:, 2048 * q:2048 * (q + 1)] = res[i]["h2o"][:, :2048]
        if has_ctx:
            x1C[b][:, 64 * q:64 * (q + 1)] = res[i]["x1o"][:, 2048:]
            h2C[b][:, 64 * q:64 * (q + 1)] = res[i]["h2o"][:, 2048:]
    return x1L, h2L, x1C, h2C


def build_C2(Wd, R):
    NH = (R + 2) * Wd
    NI = R * Wd
    nc = bass.Bass("TRN2", target_bir_lowering=False)
    p = Prog(nc)
    h2h = din(nc, "h2h", [1024, NH]); x1 = din(nc, "x1", [1024, NI])
    wg = din(nc, "wg", [1024, 2816]); wu = din(nc, "wu", [1024, 2816]); wd = din(nc, "wd", [2816, 1024])
    cwt = din(nc, "cwt", [128, 22, 9]); cbt = din(nc, "cbt", [128, 22]); gate2 = din(nc, "gate2", [128, 8])
    x2o = dout(nc, "x2o", [1024, NI])
    psr = PsRot(p, 7)
    h2b = p.sb('h2b', [128, 8, NH], BF16)
    act = p.sb('act', [128, 22, NI], BF16)
    a_s = p.sb('a_s', [128, NH]); acc = p.sb('acc', [128, NI])
    cws = p.sb('cws', [128, 22, 9]); cbs = p.sb('cbs', [128, 22]); g2s = p.sb('g2s', [128, 8])
    wst = [p.sb('wst%d' % i, [128, 8, 256]) for i in range(2)]
    wbg = [p.sb('wbg%d' % i, [128, 8, 256], BF16) for i in range(2)]
    wbu = [p.sb('wbu%d' % i, [128, 8, 256], BF16) for i in range(2)]
    wdst = [p.sb('wdst0', [128, 22, 128])] * 2
    wdbf = [p.sb('wdbf%d' % i, [128, 22, 128], BF16) for i in range(2)]
    xt = [p.sb('xt%d' % i, [128, 512]) for i in range(3)]
    p.dma(cws, cwt); p.dma(cbs, cbt); p.dma(g2s, gate2)
    hv = h2h.re("(k p) n -> p k n", p=128)
    for k in range(8):
        p.dma(a_s, hv[:, k, :])
        p.copy(h2b[:, k, :], a_s, eng='pool' if k % 2 else 'dve')
    wgv = wg.re("(k p) n -> p k n", p=128); wuv = wu.re("(k p) n -> p k n", p=128)
    for bi, (c0, cn) in enumerate(tiles(2816, 256)):
        p.dma(wst[0][:, :, :cn], wgv[:, :, c0:c0 + cn])
        p.copy(wbg[bi % 2][:, :, :cn], wst[0][:, :, :cn], eng='pool')
        p.dma(wst[1][:, :, :cn], wuv[:, :, c0:c0 + cn])
        p.copy(wbu[bi % 2][:, :, :cn], wst[1][:, :, :cn], eng='pool')
        for fi in range(cn // 128):
            f = c0 // 128 + fi
            fs = slice(128 * fi, 128 * (fi + 1))
            for (t0, tn) in tiles(NH, 512):
                ps = psr.get()
                for k in range(8):
                    p.mm(ps[:, :tn], wbg[bi % 2][:, k, fs], h2b[:, k, t0:t0 + tn], start=(k == 0), stop=(k == 7))
                p.copy(a_s[:, t0:t0 + tn], ps[:, :tn], eng='act')
            p.ts(acc, a_s[:, Wd:Wd + NI], cws[:, f, 4:5], ALU.mult, cbs[:, f:f + 1], ALU.add)
            for dr in (-1, 0, 1):
                for dw in (-1, 0, 1):
                    if dr == 0 and dw == 0:
                        continue
                    w0 = max(0, -dw); w1 = Wd - max(0, dw); cw_ = w1 - w0
                    tap = (dr + 1) * 3 + (dw + 1)
                    ov = acc.view(w0, [[Wd, R], [1, cw_]])
                    sv = a_s.view((1 + dr) * Wd + w0 + dw, [[Wd, R], [1, cw_]])
                    p.stt(ov, sv, cws[:, f, tap:tap + 1], ov, ALU.mult, ALU.add)
            p.act(acc, acc, ACT.Silu)
            for (t0, tn) in tiles(NI, 512):
                ps = psr.get()
                for k in range(8):
                    p.mm(ps[:, :tn], wbu[bi % 2][:, k, fs], h2b[:, k, Wd + t0:Wd + t0 + tn], start=(k == 0), stop=(k == 7))
                p.tt(act[:, f, t0:t0 + tn], ps[:, :tn], acc[:, t0:t0 + tn], ALU.mult)
    x1v = x1.re("(k p) n -> p k n", p=128); x2v = x2o.re("(k p) n -> p k n", p=128)
    st = {'n': 0}

    def evac(m0, mn, t0, tn, ps):
        m = m0 // 128
        i = st['n'] % 3
        st['n'] += 1
        p.dma(xt[i][:, :tn], x1v[:, m, t0:t0 + tn])
        p.stt(xt[i][:, :tn], ps[:, :tn], g2s[:, m:m + 1], xt[i][:, :tn], ALU.mult, ALU.add)
        p.dma(x2v[:, m, t0:t0 + tn], xt[i][:, :tn])
    fm_linear(p, psr, wd, 2816, 1024, act, NI, evac, wdst, wdbf, colblk=128)
    p.wait_all('sp')
    p.emit()
    return nc


def stage_C2(h2L, x1L, modl, P, ctx=False):
    cwt = np.ascontiguousarray(P['ffn_conv_w'].reshape(9, 22, 128).transpose(2, 1, 0))
    cbt = np.ascontiguousarray(P['ffn_conv_b'].reshape(22, 128).T)
    maps = []
    if not ctx:
        nc = build_C2(64, 32)
        for i in range(8):
            b, q = i // 4, i % 4
            hp = np.zeros((1024, 34 * 64), np.float32)
            lo = 2048 * q - 64; hi = 2048 * (q + 1) + 64
            s0 = max(lo, 0); s1 = min(hi, 8192)
            hp[:, s0 - lo:s1 - lo] = h2L[b][:, s0:s1]
            maps.append({"h2h": hp, "x1": np.ascontiguousarray(x1L[b][:, 2048 * q:2048 * (q + 1)]), "wg": P['ffn_w_gate'],
                         "wu": P['ffn_w_up'], "wd": P['ffn_w_down'], "cwt": cwt, "cbt": cbt,
                         "gate2": modvec(modl[5120:6144, b])})
        res = run(nc, maps)
        x2 = np.zeros((2, 1024, 8192), np.float32)
        for i in range(8):
            b, q = i // 4, i % 4
            x2[b][:, 2048 * q:2048 * (q + 1)] = res[i]["x2o"]
        return x2
    nc = build_C2(256, 1)
    for b in range(2):
        hp = np.zeros((1024, 768), np.float32)
        hp[:, 256:512] = h2L[b]
        maps.append({"h2h": hp, "x1": np.ascontiguousarray(x1L[b]), "wg": P['ffn_w_gate'], "wu": P['ffn_w_up'],
                     "wd": P['ffn_w_down'], "cwt": cwt, "cbt": cbt, "gate2": modvec(modl[5120:6144, 2])})
    res = run(nc, maps, ncores=2)
    return np.stack([res[0]["x2o"], res[1]["x2o"]])


def build_F(N):
    nc = bass.Bass("TRN2", target_bir_lowering=False)
    p = Prog(nc)
    xT = din(nc, "xT", [1024, N]); g = din(nc, "g", [128, 8]); o = dout(nc, "o", [1024, N])
    ones = load_consts(p, nc)
    psr = PsRot(p, 4)
    xs = p.sb('xs', [128, 8, N]); ho = p.sb('ho', [128, 8, N])
    gm = p.sb('gm', [128, 1, 8]); sh = p.sb('sh', [128, 1, 8])
    rstd = p.sb('rstd', [128, 512])
    sqb = [p.sb('sq%d' % i, [128, 512]) for i in range(2)]
    tmpb = [p.sb('tb%d' % i, [128, 512]) for i in range(2)]
    p.dma(gm[:, 0, :], g)
    p.memset(sh, 0.0)
    xv = xT.re("(k p) n -> p k n", p=128); ov = o.re("(k p) n -> p k n", p=128)
    for k in range(8):
        p.dma(xs[:, k, :], xv[:, k, :])
    fm_norm_mod(p, psr, xs, 8, [(0, N, 0)], ones, gm, sh, ho, rstd, sqb, tmpb, 1024.0)
    for k in range(8):
        p.dma(ov[:, k, :], ho[:, k, :])
    p.wait_all('sp')
    p.emit()
    return nc


def stage_F(xT_lat, g):
    nc = build_F(2048)
    maps = []
    for i in range(8):
        b, q = i // 4, i % 4
        maps.append({"xT": np.ascontiguousarray(xT_lat[b][:, 2048 * q:2048 * (q + 1)]), "g": modvec(g)})
    res = run(nc, maps)
    out = np.zeros((2, 8192, 1024), np.float32)
    for i in range(8):
        b, q = i // 4, i % 4
        out[b, 2048 * q:2048 * (q + 1), :] = res[i]["o"].T
    return out


LAYER_KEYS = ["w_in", "w_out", "hg_norm_g", "hy_conv_w", "hy_conv_b", "hy_w1", "hy_b1", "hy_freq1", "hy_w2", "hy_b2",
              "hy_freq2", "hy_w3", "hy_bias", "ssd_conv_w", "ssd_conv_b", "ssd_dt_bias", "ssd_a_log", "ssd_d",
              "ssd_norm_g", "ffn_w_gate", "ffn_w_up", "ffn_conv_w", "ffn_conv_b", "ffn_w_down", "norm1_g", "norm2_g"]


def kernel(**inp):
    inp = {k: np.asarray(v, dtype=np.float32) for k, v in inp.items()}
    mods = stage_M(inp['c'], inp['c_ctx'], inp['w_ada'], inp['b_ada'])
    xT = np.ascontiguousarray(inp['x'].transpose(0, 2, 1))
    cT = np.ascontiguousarray(inp['ctx'].transpose(0, 2, 1))
    for l in range(2):
        P = {k: np.ascontiguousarray(inp[k][l]) for k in LAYER_KEYS}
        last = l == 1
        uL, uC = stage_A(xT, cT, mods[l], P['norm1_g'], P['w_in'])
        oL, oC = stage_B1(uL, uC, inp['hg_lb_logits'], float(l))
        yL, yC, xsL, xsC = stage_B2(uL, uC, P['ssd_conv_w'], P['ssd_conv_b'], P['ssd_dt_bias'], P['ssd_a_log'])
        ohL = stage_B3(np.ascontiguousarray(uL[:, 1280:2048, :]), 8192, P)
        if not last:
            ohC = stage_B3(np.ascontiguousarray(uC[:, 1280:2048, :]), 256, P)
            x1L, h2L, x1C, h2C = stage_C1(xT, cT, uL, uC, oL, oC, yL, yC, xsL, xsC, ohL, ohC, mods[l], P)
            cT = stage_C2(h2C, x1C, mods[l], P, ctx=True)
        else:
            x1L, h2L, _, _ = stage_C1(xT, None, uL, None, oL, None, yL, None, xsL, None, ohL, None, mods[l], P)
        xT = stage_C2(h2L, x1L, mods[l], P)
    return stage_F(xT, inp['final_norm_g'])
```

```python
import contextlib
import numpy as np
import concourse.bass as bass
import concourse.mybir as mybir
from concourse.bass_utils import run_bass_kernel_spmd

F32 = mybir.dt.float32
BF16 = mybir.dt.bfloat16
ACT = mybir.ActivationFunctionType
ALU = mybir.AluOpType
AX = mybir.AxisListType
AP = bass.AP

ENG = ['pe', 'act', 'dve', 'pool', 'sp']
NDMA = 8


class T:
    def __init__(self, ap, key):
        self.ap = ap
        self.key = key

    def __getitem__(self, idx):
        return T(self.ap[idx], self.key)

    def k(self, sub):
        return T(self.ap, (self.key[0], sub))

    def re(self, s, **kw):
        return T(self.ap.rearrange(s, **kw), self.key)

    def view(self, off, dims, np_=None):
        a = self.ap
        pd = list(a.ap[0])
        if np_ is not None:
            pd = [pd[0], np_]
        return T(AP(a.tensor, a.offset + off, [pd] + [list(d) for d in dims]), self.key)

    def bc(self, dt):
        return T(self.ap.bitcast(dt), self.key)


class Prog:
    def __init__(self, nc, same_engine_sync=True):
        self.nc = nc
        self.es = contextlib.ExitStack()
        self.engs = {'pe': nc.tensor, 'act': nc.scalar, 'dve': nc.vector, 'pool': nc.gpsimd, 'sp': nc.sync}
        self.q = {e: [] for e in ENG}
        self.sems = {}
        self.cnt = {}
        for e in ENG[:4]:
            self.sems[e] = self.es.enter_context(nc.semaphore('s_' + e))
            self.cnt[e] = 0
        for i in range(NDMA):
            self.sems['d%d' % i] = self.es.enter_context(nc.semaphore('s_d%d' % i))
            self.cnt['d%d' % i] = 0
        self.seen = {e: {} for e in ENG}
        self.tab = {}
        self.ndma = 0
        self.ses = same_engine_sync
        self.nuniq = 0

    def sb(self, name, shape, dt=F32):
        t = self.es.enter_context(self.nc.sbuf_tensor(name, list(shape), dt))
        return T(t[:] if len(shape) == 2 else t[tuple(slice(None) for _ in shape)], (name, 0))

    def ps(self, name, shape, dt=F32):
        t = self.es.enter_context(self.nc.psum_tensor(name, list(shape), dt))
        return T(t[:], (name, 0))

    def dram(self, name, shape, dt=F32, kind="Internal"):
        t = self.nc.dram_tensor(name, list(shape), dt, kind=kind)
        return T(t.ap(), (name, 0))

    def _recs(self, key):
        name, sub = key
        d = self.tab.setdefault(name, {})
        if sub == 0:
            return list(d.values())
        out = []
        if 0 in d:
            out.append(d[0])
        if sub in d:
            out.append(d[sub])
        return out

    def _deps(self, eng, reads, writes):
        need = {}

        def add(src):
            if src is None:
                return
            s, v = src
            if (not self.ses or eng == 'pe') and s == eng:
                return
            if need.get(s, 0) < v:
                need[s] = v
        for r in reads:
            for rec in self._recs(r.key):
                add(rec[0])
        for w in writes:
            for rec in self._recs(w.key):
                add(rec[0])
                for rd in rec[1]:
                    add(rd)
        out = []
        for s, v in need.items():
            if self.seen[eng].get(s, 0) < v:
                self.seen[eng][s] = v
                out.append((s, v))
        return out

    def _record(self, src, reads, writes):
        for r in reads:
            name, sub = r.key
            d = self.tab.setdefault(name, {})
            rec = d.setdefault(sub, [None, []])
            rec[1].append(src)
            if len(rec[1]) > 64:
                mx = {}
                for s, v in rec[1]:
                    mx[s] = max(mx.get(s, 0), v)
                rec[1] = list(mx.items())
        for w in writes:
            name, sub = w.key
            d = self.tab.setdefault(name, {})
            if sub == 0:
                rec = d.setdefault(0, [None, []])
                rec[0] = src
                rec[1] = []
            else:
                rec = d.setdefault(sub, [None, []])
                rec[0] = src
                rec[1] = []

    def op(self, eng, fn, reads, writes):
        waits = self._deps(eng, reads, writes)
        self.cnt[eng] += 1
        v = self.cnt[eng]
        sem = self.sems[eng]
        self.q[eng].append((waits, fn, sem, 1))
        self._record((eng, v), reads, writes)

    def dma(self, out, in_, eng='sp', **kw):
        k = 'd%d' % (self.ndma % NDMA)
        self.ndma += 1
        waits = self._deps(eng, [in_], [out])
        prev = self.cnt[k]
        if prev > 0 and self.seen[eng].get(k, 0) < prev:
            self.seen[eng][k] = prev
            waits.append((k, prev))
        self.cnt[k] += 16
        v = self.cnt[k]
        o, i = out.ap, in_.ap
        self.q[eng].append((waits, lambda e: e.dma_start(out=o, in_=i, **kw), self.sems[k], 16))
        self._record((k, v), [in_], [out])

    def wait_all(self, eng='sp'):
        waits = []
        for s, v in self.cnt.items():
            if v > 0 and self.seen[eng].get(s, 0) < v:
                self.seen[eng][s] = v
                waits.append((s, v))
        self.q[eng].append((waits, None, None, 0))

    def emit(self):
        nc = self.nc
        with nc.Block() as block:
            def run(e, name):
                for waits, fn, sem, inc in self.q[name]:
                    for s, v in waits:
                        e.wait_ge(self.sems[s], v)
                    if fn is not None:
                        fn(e).then_inc(sem, inc)

            @block.tensor
            def _(e):
                run(e, 'pe')

            @block.scalar
            def _(e):
                run(e, 'act')

            @block.vector
            def _(e):
                run(e, 'dve')

            @block.gpsimd
            def _(e):
                run(e, 'pool')

            @block.sync
            def _(e):
                run(e, 'sp')
        self.es.close()

    def mm(self, out, lhsT, rhs, start=True, stop=True, extra_reads=()):
        o, l, r = out.ap, lhsT.ap, rhs.ap
        rd = [lhsT, rhs] + list(extra_reads)
        if not start:
            rd.append(out)
        self.op('pe', lambda e: e.matmul(o, l, r, start=start, stop=stop), rd, [out])

    def tr(self, out, in_, ident):
        o, i, d = out.ap, in_.ap, ident.ap
        self.op('pe', lambda e: e.transpose(o, i, d), [in_, ident], [out])

    def act(self, out, in_, func, bias=None, scale=None, accum=None, eng='act'):
        o, i = out.ap, in_.ap
        kw = {}
        rd = [in_]
        wr = [out]
        if bias is not None:
            if isinstance(bias, T):
                kw['bias'] = bias.ap
                rd.append(bias)
            else:
                kw['bias'] = bias
        if scale is not None:
            if isinstance(scale, T):
                kw['scale'] = scale.ap
                rd.append(scale)
            else:
                kw['scale'] = scale
        if accum is not None:
            kw['accum_out'] = accum.ap
            wr.append(accum)
        self.op('act', lambda e: e.activation(o, i, func, **kw), rd, wr)

    def tt(self, out, a, b, op, eng='dve'):
        o, x, y = out.ap, a.ap, b.ap
        self.op(eng, lambda e: e.tensor_tensor(o, x, y, op), [a, b], [out])

    def ts(self, out, a, s1, op0, s2=None, op1=None, eng='dve', accum=None):
        o, x = out.ap, a.ap
        rd = [a]
        wr = [out]
        v1 = s1.ap if isinstance(s1, T) else s1
        v2 = s2.ap if isinstance(s2, T) else s2
        if isinstance(s1, T):
            rd.append(s1)
        if isinstance(s2, T):
            rd.append(s2)
        kw = {}
        if op1 is not None:
            kw['op1'] = op1
        if accum is not None:
            kw['accum_out'] = accum.ap
            wr.append(accum)
        self.op(eng, lambda e: e.tensor_scalar(o, x, v1, v2, op0, **kw), rd, wr)

    def stt(self, out, a, s, b, op0, op1, accum=None):
        o, x, y = out.ap, a.ap, b.ap
        rd = [a, b]
        wr = [out]
        sv = s.ap if isinstance(s, T) else s
        if isinstance(s, T):
            rd.append(s)
        kw = {}
        if accum is not None:
            kw['accum_out'] = accum.ap
            wr.append(accum)
        self.op('dve', lambda e: e.scalar_tensor_tensor(o, x, sv, y, op0, op1, **kw), rd, wr)

    def scan(self, out, d0, d1, init, op0=ALU.mult, op1=ALU.add):
        o, x, y = out.ap, d0.ap, d1.ap
        rd = [d0, d1]
        iv = init.ap if isinstance(init, T) else init
        if isinstance(init, T):
            rd.append(init)
        self.op('dve', lambda e: e.tensor_tensor_scan(o, x, y, iv, op0, op1), rd, [out])

    def copy(self, out, in_, eng='dve'):
        o, i = out.ap, in_.ap
        if eng == 'act':
            self.op('act', lambda e: e.copy(o, i), [in_], [out])
        else:
            self.op(eng, lambda e: e.tensor_copy(o, i), [in_], [out])

    def memset(self, out, val, eng='dve'):
        o = out.ap
        self.op(eng, lambda e: e.memset(o, val), [], [out])

    def recip(self, out, in_):
        o, i = out.ap, in_.ap
        self.op('dve', lambda e: e.reciprocal(o, i), [in_], [out])

    def reduce(self, out, in_, op=ALU.add, axis=AX.X):
        o, i = out.ap, in_.ap
        self.op('dve', lambda e: e.tensor_reduce(o, i, axis, op), [in_], [out])


EPS = 1e-6


def din(nc, name, shape, dt=F32):
    return T(nc.dram_tensor(name, list(shape), dt, kind="ExternalInput").ap(), (name, 0))


def dout(nc, name, shape, dt=F32):
    return T(nc.dram_tensor(name, list(shape), dt, kind="ExternalOutput").ap(), (name, 0))


def tiles(n, sz):
    return [(s, min(sz, n - s)) for s in range(0, n, sz)]


class PsRot:
    def __init__(self, p, n=6, pref='psr'):
        self.t = [p.ps('%s%d' % (pref, i), [128, 512]) for i in range(n)]
        self.i = 0

    def get(self):
        t = self.t[self.i % len(self.t)]
        self.i += 1
        return t


def run(nc, in_maps, ncores=8):
    res = run_bass_kernel_spmd(nc, in_maps, core_ids=list(range(ncores)))
    return res.results


def build_M():
    nc = bass.Bass("TRN2", target_bir_lowering=False)
    p = Prog(nc)
    cT = din(nc, "cT", [128, 24])
    w = din(nc, "w", [1024, 1536])
    b = din(nc, "b", [128, 12])
    o = dout(nc, "o", [128, 36])
    cs = p.sb('cs', [128, 24]); sc = p.sb('sc', [128, 24])
    ws = p.sb('ws', [128, 8, 1536]); bs = p.sb('bs', [128, 12]); os_ = p.sb('os', [128, 36])
    ps = p.ps('ps', [128, 512])
    p.dma(cs, cT); p.dma(bs, b)
    wv = w.re("(k p) n -> p k n", p=128)
    for k in range(8):
        p.dma(ws[:, k, :].k(('k', k)), wv[:, k, :])
    p.act(sc, cs, ACT.Silu)
    for j in range(12):
        for k in range(8):
            p.mm(ps[:, 3 * j:3 * j + 3], ws[:, k, 128 * j:128 * (j + 1)].k(('k', k)), sc[:, 3 * k:3 * k + 3],
                 start=(k == 0), stop=(k == 7))
    p.tt(os_.view(0, [[3, 12], [1, 3]]), ps[:, 0:36].view(0, [[3, 12], [1, 3]]), bs.view(0, [[1, 12], [0, 3]]), ALU.add)
    p.dma(o, os_)
    p.wait_all('sp')
    p.emit()
    return nc


def stage_M(c, c_ctx, w_ada, b_ada):
    nc = build_M()
    cc = np.concatenate([c, c_ctx[None]], 0)
    cT = np.ascontiguousarray(cc.T.reshape(8, 128, 3).transpose(1, 0, 2).reshape(128, 24))
    wf = np.concatenate([w_ada[0], w_ada[1]], 1)
    bf = np.concatenate([b_ada[0], b_ada[1]], 0)
    maps = []
    for i in range(8):
        maps.append({"cT": cT, "w": np.ascontiguousarray(wf[:, 1536 * i:1536 * (i + 1)]),
                     "b": np.ascontiguousarray(bf[1536 * i:1536 * (i + 1)].reshape(12, 128).T)})
    res = run(nc, maps)
    outs = [r["o"].reshape(128, 12, 3).transpose(1, 0, 2).reshape(1536, 3) for r in res]
    full = np.concatenate(outs, 0)
    return full.reshape(2, 6144, 3)


def fm_norm_mod(p, psr, xs, KC, segs, ones, gm, sh, hout, rstd, sqb, tmpb, nfeat):
    for (s0, sn, mi) in segs:
        for (t0, tn) in [(s0 + a, b) for a, b in tiles(sn, 512)]:
            ps = psr.get()
            for k in range(KC):
                sq = sqb[k % 2]
                p.act(sq[:, :tn], xs[:, k, t0:t0 + tn], ACT.Square)
                p.mm(ps[:, :tn], ones, sq[:, :tn], start=(k == 0), stop=(k == KC - 1))
            p.act(rstd[:, :tn], ps[:, :tn], ACT.Sqrt, scale=1.0 / nfeat, bias=p.epsc)
            p.recip(rstd[:, :tn], rstd[:, :tn])
            for k in range(KC):
                tb = tmpb[k % 2]
                p.tt(tb[:, :tn], xs[:, k, t0:t0 + tn], rstd[:, :tn], ALU.mult)
                p.ts(hout[:, k, t0:t0 + tn], tb[:, :tn], gm[:, mi, k:k + 1], ALU.mult, sh[:, mi, k:k + 1], ALU.add,
                     eng='pool' if k % 2 else 'dve')


def load_consts(p, nc):
    p.epsc = p.sb('epsc', [128, 1])
    p.memset(p.epsc, EPS)
    ones = p.sb('ones', [128, 128])
    p.memset(ones, 1.0)
    return ones


def fm_linear(p, psr, w, K, Ncols, hT, Ntok, evac, wst, wbf, colblk=512):
    KC = K // 128
    wv = w.re("(k p) n -> p k n", p=128)
    for bi, (c0, cn) in enumerate(tiles(Ncols, colblk)):
        wf = wst[bi % 2]
        wb = wbf[bi % 2]
        p.dma(wf[:, :, :cn], wv[:, :, c0:c0 + cn])
        p.copy(wb[:, :, :cn], wf[:, :, :cn], eng='pool')
        for (m0, mn) in tiles(cn, 128):
            for (t0, tn) in tiles(Ntok, 512):
                ps = psr.get()
                for k in range(KC):
                    p.mm(ps[:mn, :tn], wb[:, k, m0:m0 + mn], hT[:, k, t0:t0 + tn], start=(k == 0), stop=(k == KC - 1))
                evac(c0 + m0, mn, t0, tn, ps)


def build_A(NL, NCX, ncols=3600):
    N = NL + NCX
    nc = bass.Bass("TRN2", target_bir_lowering=False)
    p = Prog(nc)
    xT = din(nc, "xT", [1024, N])
    mod = din(nc, "mod", [128, 4, 8])
    g = din(nc, "g", [128, 8])
    w = din(nc, "w", [1024, ncols])
    uT = dout(nc, "uT", [ncols, N])
    ones = load_consts(p, nc)
    psr = PsRot(p, 7)
    xs = p.sb('xs', [128, 8, N])
    hT = p.sb('hT', [128, 8, N], BF16)
    mods = p.sb('mods', [128, 4, 8]); gs = p.sb('gs', [128, 8])
    gm = p.sb('gm', [128, 2, 8]); sh = p.sb('sh', [128, 2, 8])
    rstd = p.sb('rstd', [128, 512])
    sqb = [p.sb('sq%d' % i, [128, 512]) for i in range(2)]
    tmpb = [p.sb('tb%d' % i, [128, 512]) for i in range(2)]
    wst = [p.sb('wst%d' % i, [128, 8, 512]) for i in range(2)]
    wbf = [p.sb('wbf%d' % i, [128, 8, 512], BF16) for i in range(2)]
    ost = [p.sb('ost%d' % i, [128, N]) for i in range(2)]
    p.dma(mods, mod); p.dma(gs, g)
    xv = xT.re("(k p) n -> p k n", p=128)
    for k in range(8):
        p.dma(xs[:, k, :], xv[:, k, :])
    for i in range(2):
        p.ts(gm[:, i, :], mods[:, 2 * i + 1, :], 1.0, ALU.add)
        p.tt(gm[:, i, :], gm[:, i, :], gs, ALU.mult)
        p.copy(sh[:, i, :], mods[:, 2 * i, :])
    segs = [(0, NL, 0)] + ([(NL, NCX, 1)] if NCX else [])
    fm_norm_mod(p, psr, xs, 8, segs, ones, gm, sh, hT, rstd, sqb, tmpb, 1024.0)
    state = {'n': 0}

    def evac(m0, mn, t0, tn, ps):
        ob = ost[(m0 // 128) % 2]
        if state['n'] % 2:
            p.copy(ob[:mn, t0:t0 + tn], ps[:mn, :tn], eng='act')
        else:
            p.copy(ob[:mn, t0:t0 + tn], ps[:mn, :tn], eng='dve')
        state['n'] += 1
        if t0 + tn == N:
            p.dma(uT[m0:m0 + mn, :], ob[:mn, :])
    fm_linear(p, psr, w, 1024, ncols, hT, N, evac, wst, wbf)
    p.wait_all('sp')
    p.emit()
    return nc


def modvec(v):
    return np.ascontiguousarray(v.reshape(8, 128).T)


def stage_A(xT_lat, xT_ctx, modl, g1, w_in):
    NCX = 64 if xT_ctx is not None else 0
    nc = build_A(2048, NCX)
    maps = []
    for i in range(8):
        b, q = i // 4, i % 4
        xin = xT_lat[b][:, 2048 * q:2048 * (q + 1)]
        if NCX:
            xin = np.concatenate([xin, xT_ctx[b][:, 64 * q:64 * (q + 1)]], 1)
        mod = np.stack([modvec(modl[0:1024, b]), modvec(modl[1024:2048, b]),
                        modvec(modl[0:1024, 2]), modvec(modl[1024:2048, 2])], 1)
        maps.append({"xT": np.ascontiguousarray(xin), "mod": np.ascontiguousarray(mod), "g": modvec(g1), "w": w_in})
    res = run(nc, maps)
    uL = np.zeros((2, 3600, 8192), np.float32)
    uC = np.zeros((2, 3600, 256), np.float32) if NCX else None
    for i in range(8):
        b, q = i // 4, i % 4
        u = res[i]["uT"]
        uL[b][:, 2048 * q:2048 * (q + 1)] = u[:, :2048]
        if NCX:
            uC[b][:, 64 * q:64 * (q + 1)] = u[:, 2048:]
    return uL, uC


def build_B1(T_, nitem=2, TT=1408):
    NB = T_ // 128
    NCH = T_ // 32
    nc = bass.Bass("TRN2", target_bir_lowering=False)
    p = Prog(nc)
    qTs = [din(nc, "qT%d" % i, [64, T_]) for i in range(nitem)]
    fTs = [din(nc, "fT%d" % i, [64, T_]) for i in range(nitem)]
    vts = [din(nc, "vt%d" % i, [128, NB, 64]) for i in range(nitem)]
    vhs = [din(nc, "vh%d" % i, [64, 2 * NB, 64]) for i in range(nitem)]
    lbs = [din(nc, "lb%d" % i, [64, 3]) for i in range(nitem)]
    maskd = din(nc, "mask", [128, 128])
    identd = din(nc, "ident", [64, 64])
    oTs = [dout(nc, "oT%d" % i, [64, T_]) for i in range(nitem)]
    psr = PsRot(p, 4)
    pst = p.ps('pst', [128, 1024], BF16)
    psu2 = [p.ps('psu%d' % i, [128, 512]) for i in range(2)]
    A = p.sb('A', [64, TT]); Kk = p.sb('K', [64, TT]); B = p.sb('B', [64, TT]); Q = p.sb('Q', [64, TT])
    E1 = p.sb('E1', [64, TT]); E2 = p.sb('E2', [64, TT])
    msk = p.sb('msk', [64, TT])
    kd = p.sb('kd', [64, T_], BF16); qd = p.sb('qd', [64, T_], BF16); qb = p.sb('qb', [64, T_], BF16)
    kdec = p.sb('kdec', [64, T_], BF16)
    dch = p.sb('dch', [64, NCH])
    big = p.sb('big', [128, T_])
    vf = big[:, 0:NB * 64].re("p (j c) -> p j c", c=64); vb = p.sb('vb', [128, NB, 64], BF16)
    vhf = big[0:64, 0:2 * NB * 64].re("p (j c) -> p j c", c=64); vhb = p.sb('vhb', [64, 2 * NB, 64], BF16)
    Sall = p.sb('Sall', [64, NCH + 1, 64], BF16)
    S32 = [p.sb('S32_%d' % i, [64, 64]) for i in range(2)]
    kdt = [p.sb('kdt%d' % i, [64, 2, 64], BF16) for i in range(2)]
    scT = [p.sb('scT%d' % i, [128, 128], BF16) for i in range(2)]
    oT = big[0:64, :]
    mask = p.sb('mask_s', [128, 128]); identf = p.sb('identf', [64, 64]); ident = p.sb('ident_s', [64, 64], BF16)
    lb = p.sb('lb_s', [64, 1]); oml = p.sb('oml', [64, 1]); lb3 = p.sb('lb3', [64, 3])
    p.dma(mask, maskd); p.dma(identf, identd)
    p.copy(ident, identf)
    p.memset(msk, 1.0)
    p.memset(msk.view(0, [[32, TT // 32]]), 0.0)
    SC = 64 ** -0.5
    ncht = TT // 32
    for it in range(nitem):
        p.dma(lb3, lbs[it])
        p.tt(lb, lb3[:, 1:2], lb3[:, 0:1], ALU.subtract)
        p.act(lb, lb, ACT.Sigmoid)
        p.tt(lb, lb, lb3[:, 2:3], ALU.mult)
        p.ts(oml, lb, -1.0, ALU.mult, 1.0, ALU.add)
        p.dma(vf, vts[it])
        p.copy(vb, vf, eng='pool')
        p.dma(vhf, vhs[it])
        p.copy(vhb, vhf, eng='pool')
        for ti, (t0, tn) in enumerate(tiles(T_, TT)):
            p.dma(A, fTs[it][:, t0:t0 + tn]); p.dma(Q, qTs[it][:, t0:t0 + tn])
            p.act(Kk, A, ACT.Sigmoid, scale=-1.0)
            p.ts(Kk, Kk, oml, ALU.mult)
            p.act(E1, A, ACT.Sigmoid)
            p.ts(E1, E1, oml, ALU.mult, lb, ALU.add)
            p.act(A, E1, ACT.Ln)
            p.scan(B, msk, A, 0.0)
            bend = B.view(31, [[32, ncht]])
            c0 = t0 // 32
            p.act(dch[:, c0:c0 + ncht], bend, ACT.Exp)
            p.tt(E1.view(0, [[32, ncht], [1, 32]]), B.view(0, [[32, ncht], [1, 32]]), B.view(15, [[32, ncht], [0, 32]]), ALU.subtract)
            p.act(E2, E1, ACT.Exp)
            p.stt(qd[:, t0:t0 + tn], E2, SC, Q, ALU.mult, ALU.mult)
            p.act(E2, E1, ACT.Exp, scale=-1.0)
            p.tt(kd[:, t0:t0 + tn], Kk, E2, ALU.mult)
            p.tt(E1.view(0, [[32, ncht], [1, 32]]), B.view(0, [[32, ncht], [1, 32]]), B.view(31, [[32, ncht], [0, 32]]), ALU.subtract)
            p.act(E2, E1, ACT.Exp, scale=-1.0)
            p.tt(kdec[:, t0:t0 + tn], Kk, E2, ALU.mult)
            p.act(E2, B, ACT.Exp)
            p.stt(qb[:, t0:t0 + tn], E2, SC, Q, ALU.mult, ALU.mult)
        p.memset(Sall[:, 0, :], 0.0)
        p.memset(S32[0], 0.0)
        sidx = 0
        for j in range(NB):
            kt = kdt[j % 2]
            for hb in range(2):
                p.tr(pst[0:64, 64 * hb:64 * (hb + 1)], kdec[:, 128 * j + 64 * hb:128 * j + 64 * (hb + 1)], ident)
            p.copy(kt, pst[0:64, 0:128].view(0, [[64, 2], [1, 64]]))
            for c in range(4):
                hb, c2 = c // 2, c % 2
                p.mm(psu2[c2][0:64, 64 * hb:64 * (hb + 1)], kt[32 * c2:32 * (c2 + 1), hb, :], vhb[32 * c2:32 * (c2 + 1), 2 * j + hb, :])
            for c in range(4):
                ch = 4 * j + c
                sc_, sn_ = S32[sidx % 2], S32[(sidx + 1) % 2]
                p.stt(sn_, sc_, dch[:, ch:ch + 1], psu2[c % 2][0:64, 64 * (c // 2):64 * (c // 2 + 1)], ALU.mult, ALU.add)
                p.copy(Sall[:, ch + 1, :], sn_, eng='act')
                sidx += 1
        for j in range(NB):
            ps_s = psr.get()
            blk = slice(128 * j, 128 * (j + 1))
            p.mm(ps_s[:, 0:128], kd[:, blk], qd[:, blk])
            st = scT[j % 2]
            p.tt(st, ps_s[:, 0:128], mask, ALU.mult)
            ps_o = psr.get()
            p.mm(ps_o[0:64, 0:128], vb[:, j, :], st, start=True, stop=False)
            for c in range(4):
                ch = 4 * j + c
                p.mm(ps_o[0:64, 32 * c:32 * (c + 1)], Sall[:, ch, :], qb[:, 128 * j + 32 * c:128 * j + 32 * (c + 1)],
                     start=False, stop=(c == 3))
            p.copy(oT[:, blk], ps_o[0:64, 0:128], eng='act')
        p.dma(oTs[it], oT)
    p.wait_all('sp')
    p.emit()
    return nc


def hg_mask():
    s = np.arange(128)[:, None]; t = np.arange(128)[None, :]
    return ((s // 32 == t // 32) & (s <= t)).astype(np.float32)


def seq_cat(uc, ul, rev):
    if rev:
        return np.concatenate([uc[:, ::-1], ul[:, ::-1]], 1)
    return np.concatenate([uc, ul], 1)


def seq_split(o, rev):
    oc, ol = o[:, :256], o[:, 256:]
    if rev:
        return oc[:, ::-1], ol[:, ::-1]
    return oc, ol


def stage_B1(uL, uC, lbl, flag):
    T_ = 8448
    nc = build_B1(T_)
    items = [(b, h, d) for b in range(2) for h in range(4) for d in range(2)]
    maps = []
    for i in range(8):
        m = {"mask": hg_mask(), "ident": np.eye(64, dtype=np.float32)}
        for s in range(2):
            b, h, d = items[2 * i + s]
            cq = slice(64 * h, 64 * h + 64)
            cf = slice(256 + 256 * d + 64 * h, 256 + 256 * d + 64 * h + 64)
            cv = slice(768 + 64 * h, 768 + 64 * h + 64)
            m["qT%d" % s] = np.ascontiguousarray(seq_cat(uC[b][cq], uL[b][cq], d))
            m["fT%d" % s] = np.ascontiguousarray(seq_cat(uC[b][cf], uL[b][cf], d))
            v = seq_cat(uC[b][cv], uL[b][cv], d)
            m["vt%d" % s] = np.ascontiguousarray(v.T.reshape(T_ // 128, 128, 64).transpose(1, 0, 2))
            m["vh%d" % s] = np.ascontiguousarray(v.T.reshape(T_ // 64, 64, 64).transpose(1, 0, 2))
            m["lb%d" % s] = np.ascontiguousarray(np.stack([lbl[0, 64 * h:64 * h + 64], lbl[1, 64 * h:64 * h + 64],
                                                           np.full(64, flag, np.float32)], 1).astype(np.float32))
        maps.append(m)
    res = run(nc, maps)
    oL = [np.zeros((2, 256, 8192), np.float32) for _ in range(2)]
    oC = [np.zeros((2, 256, 256), np.float32) for _ in range(2)]
    for i in range(8):
        for s in range(2):
            b, h, d = items[2 * i + s]
            oc, ol = seq_split(res[i]["oT%d" % s], d)
            oL[d][b][64 * h:64 * h + 64] = ol
            oC[d][b][64 * h:64 * h + 64] = oc
    return oL, oC


def build_B2(T_, segs, TT=1408):
    NB = T_ // 128
    nc = bass.Bass("TRN2", target_bir_lowering=False)
    p = Prog(nc)
    xbc = din(nc, "xbc", [512, T_])
    cw = din(nc, "cw", [128, 4, 3]); cb = din(nc, "cb", [128, 4])
    dtl = din(nc, "dtl", [2, 4, T_])
    hpar = din(nc, "hpar", [2, 4, 2])
    rc = din(nc, "rc", [2, 8])
    i2d = din(nc, "i2", [2, 2])
    maskd = din(nc, "maskb", [128, 128]); identd = din(nc, "ident", [128, 128])
    xso = dout(nc, "xso", [256, T_])
    yo = dout(nc, "yo", [4, 64, T_])
    psr = PsRot(p, 4)
    psu2 = [p.ps('psu%d' % i, [128, 512]) for i in range(2)]
    pst = p.ps('pst', [128, 512])
    U = p.sb('U', [128, T_]); O = p.sb('O', [128, T_])
    xst = p.sb('xst', [128, NB, 256], BF16); Bt = p.sb('Bt', [128, NB, 128], BF16)
    Bb = p.sb('Bb', [128, T_], BF16); Cb = p.sb('Cb', [128, T_], BF16)
    cws = p.sb('cws', [128, 4, 3]); cbs = p.sb('cbs', [128, 4])
    maskb = p.sb('maskb_s', [128, 128]); ident = p.sb('ident_s', [128, 128])
    rcs = p.sb('rcs', [2, 8]); i2 = p.sb('i2s', [2, 2]); hps = p.sb('hps', [2, 4, 2])
    zo = p.sb('zo', [2, 128]); av = p.sb('av', [2, 4])
    p.dma(cws, cw); p.dma(cbs, cb); p.dma(maskb, maskd); p.dma(ident, identd)
    p.dma(rcs, rc); p.dma(i2, i2d); p.dma(hps, hpar)
    p.memset(zo, 1.0)
    p.ts(zo, zo, rcs[:, 6:7], ALU.mult)
    p.act(av, hps.view(1, [[2, 4]]), ACT.Exp)
    p.ts(av, av, -1.0, ALU.mult)
    xv = xbc.re("(k p) n -> p k n", p=128)
    for k in range(4):
        p.dma(U, xv[:, k, :])
        p.ts(O, U, cws[:, k, 1:2], ALU.mult, cbs[:, k:k + 1], ALU.add)
        for (s0, sn) in segs:
            p.stt(O[:, s0 + 1:s0 + sn], U[:, s0:s0 + sn - 1], cws[:, k, 0:1], O[:, s0 + 1:s0 + sn], ALU.mult, ALU.add)
            p.stt(O[:, s0:s0 + sn - 1], U[:, s0 + 1:s0 + sn], cws[:, k, 2:3], O[:, s0:s0 + sn - 1], ALU.mult, ALU.add)
        p.act(O, O, ACT.Silu)
        if k < 2:
            p.dma(xso[128 * k:128 * (k + 1), :], O)
            for j in range(NB):
                p.tr(pst[:, 0:128], O[:, 128 * j:128 * (j + 1)], ident)
                p.copy(xst[:, j, 128 * k:128 * (k + 1)], pst[:, 0:128], eng='act' if j % 2 else 'dve')
        elif k == 2:
            p.copy(Bb, O, eng='pool')
            for j in range(NB):
                p.tr(pst[:, 0:128], O[:, 128 * j:128 * (j + 1)], ident)
                p.copy(Bt[:, j, :], pst[:, 0:128], eng='act' if j % 2 else 'dve')
        else:
            p.copy(Cb, O, eng='pool')
    R = {n: p.sb('r_' + n, [2, TT]) for n in ['x', 'dt', 'acs', 'e', 'LT', 'RT', 'DW', 'msk']}
    p.memset(R['msk'], 1.0)
    p.memset(R['msk'].view(0, [[64, TT // 64]]), 0.0)
    S32 = [[p.sb('S32_%d_%d' % (h, i), [128, 64]) for i in range(2)] for h in range(4)]
    for h in range(4):
        p.memset(S32[h][0], 0.0)
    DWt2 = [p.sb('DWt%d' % i, [128, 2]) for i in range(2)]
    ea2 = [p.sb('ea%d' % i, [128, 128]) for i in range(2)]; Cdec2 = [p.sb('Cdec%d' % i, [128, 128], BF16) for i in range(2)]
    sg2 = [p.sb('sg%d' % i, [128, 128]) for i in range(2)]; dec2 = [p.sb('dec%d' % i, [128, 128], BF16) for i in range(2)]
    MT2 = [p.sb('MT%d' % i, [128, 128], BF16) for i in range(2)]
    xdt2 = [p.sb('xdt%d' % i, [128, 64], BF16) for i in range(2)]; xw2 = [p.sb('xw%d' % i, [128, 64], BF16) for i in range(2)]
    Sb4 = [[p.sb('Sb%d_%d' % (i, c), [128, 64], BF16) for c in range(2)] for i in range(2)]
    itn = 0
    yst = [U[0:64, i * TT:(i + 1) * TT].k(('y', i)) for i in range(2)]
    ncht = TT // 64
    nblk = TT // 128
    for ti, (t0, tn) in enumerate(tiles(T_, TT)):
        for h in range(4):
            p.dma(R['x'], dtl[:, h, t0:t0 + tn])
            p.act(R['e'], R['x'], ACT.Exp, bias=hps[:, h, 0:1])
            p.act(R['dt'], R['e'], ACT.Ln, bias=1.0)
            p.ts(R['e'], R['dt'], av[:, h:h + 1], ALU.mult)
            p.scan(R['acs'], R['msk'], R['e'], 0.0)
            p.ts(R['LT'], R['acs'], rcs[:, 0:1], ALU.mult, rcs[:, 1:2], ALU.add)
            p.ts(R['RT'], R['acs'], rcs[:, 2:3], ALU.mult, rcs[:, 3:4], ALU.add)
            p.tt(R['e'].view(0, [[64, ncht], [1, 64]]), R['acs'].view(63, [[64, ncht], [0, 64]]),
                 R['acs'].view(0, [[64, ncht], [1, 64]]), ALU.subtract)
            p.act(R['e'], R['e'], ACT.Exp)
            p.ts(R['e'], R['e'], rcs[:, 4:5], ALU.mult, rcs[:, 5:6], ALU.add)
            p.tt(R['DW'], R['dt'], R['e'], ALU.mult)
            ys = yst[h % 2]
            for jb in range(nblk):
                j = t0 // 128 + jb
                DWt, ea, Cdec, sg, dec, MT, xdt, xw, Sb = (DWt2[itn % 2], ea2[itn % 2], Cdec2[itn % 2], sg2[itn % 2],
                                                        dec2[itn % 2], MT2[itn % 2], xdt2[itn % 2], xw2[itn % 2], Sb4[itn % 2])
                itn += 1
                lb_ = slice(128 * jb, 128 * (jb + 1))
                gb = slice(128 * j, 128 * (j + 1))
                ps1 = psr.get()
                p.mm(ps1[:, 0:2], R['DW'][:, lb_], i2)
                p.copy(DWt, ps1[:, 0:2], eng='act')
                p.ts(xdt, xst[:, j, 64 * h:64 * (h + 1)], DWt[:, 0:1], ALU.mult)
                p.ts(xw, xst[:, j, 64 * h:64 * (h + 1)], DWt[:, 1:2], ALU.mult, eng='pool')
                p.mm(ps1[:, 128:256], zo, R['RT'][:, lb_])
                p.act(ea, ps1[:, 128:256], ACT.Exp)
                p.tt(Cdec, Cb[:, gb], ea, ALU.mult)
                s0_, s1_ = S32[h][0], S32[h][1]
                for c in range(2):
                    p.mm(psu2[c][:, 0:64], Bt[64 * c:64 * (c + 1), j, :], xw[64 * c:64 * (c + 1), :])
                p.copy(Sb[0], s0_, eng='act')
                p.stt(s1_, s0_, ea[:, 63:64], psu2[0][:, 0:64], ALU.mult, ALU.add)
                p.copy(Sb[1], s1_, eng='act')
                p.stt(s0_, s1_, ea[:, 127:128], psu2[1][:, 0:64], ALU.mult, ALU.add)
                ps2 = psr.get()
                p.mm(ps2[:, 0:128], R['LT'][:, lb_], R['RT'][:, lb_])
                p.tt(sg, ps2[:, 0:128], maskb, ALU.add)
                p.act(dec, sg, ACT.Exp)
                p.mm(ps2[:, 128:256], Bb[:, gb], Cb[:, gb])
                p.tt(MT, ps2[:, 128:256], dec, ALU.mult)
                ps3 = psr.get()
                p.mm(ps3[0:64, 0:128], xdt, MT, start=True, stop=False)
                for c in range(2):
                    p.mm(ps3[0:64, 64 * c:64 * (c + 1)], Sb[c], Cdec[:, 64 * c:64 * (c + 1)], start=False, stop=(c == 1))
                p.copy(ys[:, lb_], ps3[0:64, 0:128], eng='act')
            p.dma(yo[h, :, t0:t0 + tn], ys)
    p.wait_all('sp')
    p.emit()
    return nc


def ssd_maskb():
    s = np.arange(128)[:, None]; t = np.arange(128)[None, :]
    return np.where((s // 64 == t // 64) & (s <= t), 0.0, -30000.0).astype(np.float32)


def stage_B2(uL, uC, cw, cbias, dt_bias, a_log):
    T_ = 8448
    nc = build_B2(T_, [(0, 256), (256, 8192)])
    items = [(b, g, d) for b in range(2) for g in range(2) for d in range(2)]
    base = 1280 + 768
    maps = []
    rc = np.array([[-1, 0, 0, 1, 0, 1, 0, 0], [0, 1, 1, 0, 1, 0, 1, 0]], np.float32)
    for i in range(8):
        b, g, d = items[i]
        cols = np.concatenate([np.arange(256 * g, 256 * g + 256), 512 + np.arange(128 * g, 128 * g + 128),
                               768 + np.arange(128 * g, 128 * g + 128)])
        ucols = base + 512 + cols
        x = seq_cat(uC[b][ucols], uL[b][ucols], d)
        w = cw[:, cols]
        if d:
            w = w[::-1]
        dcols = base + 512 + 1024 + 8 * d + 4 * g + np.arange(4)
        dl = seq_cat(uC[b][dcols], uL[b][dcols], d)
        hp = np.stack([dt_bias[d, 4 * g:4 * g + 4], a_log[d, 4 * g:4 * g + 4]], 1)
        maps.append({"xbc": np.ascontiguousarray(x),
                     "cw": np.ascontiguousarray(w.T.reshape(4, 128, 3).transpose(1, 0, 2)),
                     "cb": np.ascontiguousarray(cbias[cols].reshape(4, 128).T),
                     "dtl": np.ascontiguousarray(np.broadcast_to(dl[None], (2, 4, T_))),
                     "hpar": np.ascontiguousarray(np.broadcast_to(hp[None], (2, 4, 2))),
                     "rc": rc, "i2": np.eye(2, dtype=np.float32), "maskb": ssd_maskb(),
                     "ident": np.eye(128, dtype=np.float32)})
    res = run(nc, maps)
    yL = [np.zeros((2, 512, 8192), np.float32) for _ in range(2)]
    yC = [np.zeros((2, 512, 256), np.float32) for _ in range(2)]
    xsL = np.zeros((2, 512, 8192), np.float32); xsC = np.zeros((2, 512, 256), np.float32)
    for i in range(8):
        b, g, d = items[i]
        yc, yl = seq_split(res[i]["yo"].reshape(256, T_), d)
        yL[d][b][256 * g:256 * g + 256] = yl; yC[d][b][256 * g:256 * g + 256] = yc
        if d == 0:
            xc, xl = seq_split(res[i]["xso"], 0)
            xsL[b][256 * g:256 * g + 256] = xl; xsC[b][256 * g:256 * g + 256] = xc
    return yL, yC, xsL, xsC


MAGIC = 12582912.0
TWO_PI = 6.283185307179586


def build_B3(L, dbg=9):
    NBk = L // 128
    HL = 2 * L + 256
    OFF = L + 127
    NCc = 32
    nc = bass.Bass("TRN2", target_bir_lowering=False)
    p = Prog(nc)
    uhy = din(nc, "uhy", [96, 2, L]); cw = din(nc, "cw", [96, 3]); cb = din(nc, "cb", [96, 1])
    zT = din(nc, "zT", [33, 2, L]); tmat = din(nc, "tmat", [128, L]); ndel = din(nc, "ndel", [128, 1])
    w1 = din(nc, "w1", [33, 64]); w2 = din(nc, "w2", [64, 64]); w3 = din(nc, "w3", [64, 128])
    mlp = din(nc, "mlp", [64, 4])
    hbias = din(nc, "hbias", [64, 1]); nsel = din(nc, "nsel", [128, 1])
    Pmd = din(nc, "Pm", [128, 128]); Jd = din(nc, "J", [128, 128]); identd = din(nc, "ident", [128, 128])
    outd = dout(nc, "out", [128, NCc, 2, NBk])
    Hb = T(nc.dram_tensor("Hb", [64, HL], BF16, kind="Internal").ap(), ('Hb', 0))
    psr = PsRot(p, 6)
    F = p.sb('F', [128, L]); O1 = p.sb('O1', [128, L]); Fb = p.sb('Fb', [128, L], BF16)
    strips = [p.sb('strip%d' % i, [128, 2 * L], BF16) for i in range(2)]
    Vt = p.sb('Vt', [128, 2, NBk, NCc], BF16); X1 = p.sb('X1', [128, 2, NBk, NCc], BF16); X2 = p.sb('X2', [128, 2, NBk, NCc], BF16)
    Vr = p.sb('Vr', [128, 2 * NBk * NCc], BF16); Z = p.sb('Z', [128, NCc, 2, NBk], BF16)
    ost = [p.sb('ost%d' % i, [128, 2, NBk]) for i in range(2)]
    w1s = p.sb('w1s', [33, 64]); w2s = p.sb('w2s', [64, 64]); w3s = p.sb('w3s', [64, 128]); mlps = p.sb('mlps', [64, 4])
    hbs = p.sb('hbs', [64, 1]); nsels = p.sb('nsels', [128, 1]); ndels = p.sb('ndels', [128, 1])
    Pm = p.sb('Pms', [128, 128]); Jf = p.sb('Jf', [128, 128]); J = p.sb('Js', [128, 128], BF16); ident = p.sb('idents', [128, 128])
    cws = p.sb('cws', [96, 3]); cbs = p.sb('cbs', [96, 1])
    for a, b in [(w1s, w1), (w2s, w2), (w3s, w3), (mlps, mlp), (hbs, hbias), (nsels, nsel), (ndels, ndel), (Pm, Pmd), (Jf, Jd),
                 (ident, identd), (cws, cw), (cbs, cb)]:
        p.dma(a, b)
    p.copy(J, Jf)
    zt = p.sb('zt', [33, 512]); tm = p.sb('tm', [128, 512]); dk = p.sb('dk', [128, 512])
    ar = p.sb('ar', [64, 512]); kk = p.sb('kk', [64, 512]); h1 = p.sb('h1', [64, 512]); h2 = p.sb('h2', [64, 512])

    def sin_layer(hout, ps, bcol, fcol, n):
        p.ts(ar[:, :n], ps[0:64, :n], mlps[:, bcol:bcol + 1], ALU.add, mlps[:, fcol:fcol + 1], ALU.mult)
        p.ts(kk[:, :n], ar[:, :n], 1.0 / TWO_PI, ALU.mult, MAGIC, ALU.add)
        p.ts(kk[:, :n], kk[:, :n], MAGIC, ALU.subtract)
        p.stt(ar[:, :n], kk[:, :n], -TWO_PI, ar[:, :n], ALU.mult, ALU.add)
        p.ts(ar[:, :n], ar[:, :n], 3.14159, ALU.min, -3.14159, ALU.max)
        p.act(hout[:, :n], ar[:, :n], ACT.Sin)
    for (t0, tn) in tiles(L, 512):
        p.dma(tm[:, :tn], tmat[:, t0:t0 + tn])
        p.act(dk[:, :tn], tm[:, :tn], ACT.Exp, scale=ndels)
        for r in range(2):
            p.dma(zt[:, :tn], zT[:, r, t0:t0 + tn])
            ps = psr.get()
            p.mm(ps[0:64, :tn], w1s, zt[:, :tn])
            sin_layer(h1, ps, 0, 1, tn)
            ps = psr.get()
            p.mm(ps[0:64, :tn], w2s, h1[:, :tn])
            sin_layer(h2, ps, 2, 3, tn)
            ps = psr.get()
            p.mm(ps[:, :tn], w3s, h2[:, :tn])
            rs = slice(64 * r, 64 * (r + 1))
            p.tt(F[rs, t0:t0 + tn], ps[rs, :tn], dk[rs, :tn], ALU.mult)
    s1 = p.sb('s1', [128, 1]); a1 = p.sb('a1', [128, 1]); rinv = p.sb('rinv', [128, 1])
    p.act(O1, F, ACT.Abs, accum=s1)
    p.act(a1, F[:, L - 1:L], ACT.Abs)
    p.stt(s1, a1, nsels, s1, ALU.mult, ALU.add)
    ps = psr.get()
    p.mm(ps[:, 0:1], Pm, s1)
    p.recip(rinv, ps[:, 0:1])
    p.ts(Fb, F, rinv, ALU.mult)
    p.ts(Fb[0:64, 0:1], F[0:64, 0:1], rinv[0:64, :], ALU.mult, hbs, ALU.add)
    zb = O1.bc(BF16)[0:64, 0:2 * L]
    p.memset(zb, 0.0)
    p.dma(Hb[:, 0:L], zb[:, 0:L])
    p.dma(Hb[:, L:2 * L], zb[:, L:2 * L])
    p.dma(Hb[:, 2 * L:HL], zb[:, 0:256])
    p.dma(Hb[:, OFF:OFF + L], Fb[0:64, :])
    p.dma(Hb[:, 128:128 + L - 1], Fb[64:128, 0:L - 1])
    for b in (range(2) if dbg >= 2 else []):
        U1 = F[0:96, :]
        Oc = O1[0:96, :]
        p.dma(U1, uhy[:, b, :])
        p.ts(Oc, U1, cws[:, 1:2], ALU.mult, cbs, ALU.add)
        p.stt(Oc[:, 1:L], U1[:, 0:L - 1], cws[:, 0:1], Oc[:, 1:L], ALU.mult, ALU.add)
        p.stt(Oc[:, 0:L - 1], U1[:, 1:L], cws[:, 2:3], Oc[:, 0:L - 1], ALU.mult, ALU.add)
        for j in (range(NBk) if dbg >= 3 else []):
            ps = psr.get()
            p.tr(ps[:, 0:128], O1[:, 128 * j:128 * (j + 1)], ident)
            import os
            e3 = os.environ.get('B3E', 'dve,dve,dve').split(',')
            p.copy(Vt[:, b, j, :], ps[:, 0:32], eng=e3[0])
            p.copy(X1[:, b, j, :], ps[:, 32:64], eng=e3[1])
            p.copy(X2[:, b, j, :], ps[:, 64:96], eng=e3[2])
    W = 2 * NBk

    def reverse(dst, sf):
        tot = NCc * W
        for (c0, cn) in tiles(tot, 512):
            ps = psr.get()
            p.mm(ps[:, :cn], J, sf[:, c0:c0 + cn])
            p.copy(dst[:, c0:c0 + cn], ps[:, :cn], eng='act')
    reverse(Vr, Vt.re("p b j c -> p (b j c)"))
    ks = [0] + [k for k in range(-(NBk - 1), NBk) if k != 0]

    def conv_pass(o, inner, gate, to_out):
        for c in range(NCc):
            st = strips[c % 2]
            hoff = (o * 32 + c) * HL
            for hh in range(2):
                srcap = T(AP(Hb.ap.tensor, hoff + hh * L, [[1, 128], [1, L]]), Hb.key)
                p.dma(st[:, hh * L:(hh + 1) * L], srcap)
            ps = psr.get()
            for ki, k in enumerate(ks):
                m0 = max(0, k); m1 = min(NBk - 1, NBk - 1 + k); cnt = m1 - m0 + 1; j0 = m0 - k
                n0 = 128 * k + L
                if inner:
                    rhs = Vr.view(c + j0 * NCc, [[NBk * NCc, 2], [NCc, cnt]])
                else:
                    rhs = Vr.view(c * W + j0, [[NBk, 2], [1, cnt]])
                p.mm(ps[:, 0:W].view(m0, [[NBk, 2], [1, cnt]]), st[:, n0:n0 + 128], rhs,
                     start=(ki == 0), stop=(ki == len(ks) - 1))
            g = gate.re("p b j c -> p (b j c)").view(c, [[NBk * NCc, 2], [NCc, NBk]])
            psv = ps[:, 0:W].view(0, [[NBk, 2], [1, NBk]])
            if to_out:
                ob = ost[c % 2]
                p.tt(ob, psv, g, ALU.mult)
                p.dma(outd[:, c, :, :], ob)
            else:
                p.tt(Z[:, c, :, :], psv, g, ALU.mult)
    conv_pass(0, True, X1, False)
    reverse(Vr, Z.re("p c b j -> p (c b j)"))
    conv_pass(1, False, X2, True)
    p.wait_all('sp')
    p.emit()
    return nc


def hy_consts(L):
    t = np.linspace(0.0, 1.0, L, dtype=np.float32)[:, None]
    w = (2.0 * np.pi * np.arange(L, dtype=np.float32)[:, None] / L).astype(np.float32)
    bands = np.linspace(1e-4, 15, 16, dtype=np.float32)[None]
    z = np.concatenate([t, np.cos(bands * w), -np.sin(bands * w)], axis=-1).astype(np.float32)
    zT = np.stack([z.T, z[::-1].T], 1)
    import math
    mn = math.log(1e-2) / 1.5; mx = math.log(1e-2) / 0.3
    deltas = np.abs(np.linspace(mn, mx, 256, dtype=np.float32))
    return np.ascontiguousarray(zT), t[:, 0], deltas


def stage_B3(uhyT, L, P, dbg=9):
    nc = build_B3(L, dbg)
    zT, t, deltas = hy_consts(L)
    tmat = np.concatenate([np.broadcast_to(t[None], (64, L)), np.broadcast_to(t[::-1][None], (64, L))], 0).astype(np.float32)
    Pm = (np.arange(128)[:, None] % 64 == np.arange(128)[None, :] % 64).astype(np.float32)
    J = np.eye(128, dtype=np.float32)[::-1].copy()
    nsel = np.concatenate([np.zeros(64), -np.ones(64)]).astype(np.float32).reshape(128, 1)
    maps = []
    NBk = L // 128
    for i in range(8):
        ch = np.arange(32 * i, 32 * i + 32)
        rows = np.concatenate([ch, 256 + ch, 512 + ch])
        uh = np.stack([uhyT[0][rows], uhyT[1][rows]], 1)
        w3cols = np.concatenate([d * 512 + o * 256 + ch for d in range(2) for o in range(2)])
        ndel = -np.tile(deltas[ch], 4).reshape(128, 1).astype(np.float32)
        maps.append({"uhy": np.ascontiguousarray(uh), "cw": np.ascontiguousarray(P['hy_conv_w'][:, rows].T),
                     "cb": np.ascontiguousarray(P['hy_conv_b'][rows].reshape(96, 1)),
                     "zT": zT, "tmat": np.ascontiguousarray(tmat), "ndel": ndel,
                     "w1": P['hy_w1'], "w2": P['hy_w2'], "w3": np.ascontiguousarray(P['hy_w3'][:, w3cols]),
                     "mlp": np.ascontiguousarray(np.stack([P['hy_b1'], P['hy_freq1'], P['hy_b2'], P['hy_freq2']], 1)),
                     "hbias": np.ascontiguousarray(np.concatenate([P['hy_bias'][0][ch], P['hy_bias'][1][ch]]).reshape(64, 1)),
                     "nsel": nsel, "Pm": Pm, "J": J, "ident": np.eye(128, dtype=np.float32)})
    res = run(nc, maps)
    o = np.zeros((2, 256, L), np.float32)
    for i in range(8):
        r = res[i]["out"]
        o[:, 32 * i:32 * i + 32, :] = r.transpose(2, 1, 3, 0).reshape(2, 32, L)
    return o


def build_C1(NL, NCX):
    N = NL + NCX
    nc = bass.Bass("TRN2", target_bir_lowering=False)
    p = Prog(nc)
    xT = din(nc, "xT", [1024, N])
    hgo = din(nc, "hgo", [2, 256, N]); hgg = din(nc, "hgg", [256, N])
    sy = din(nc, "sy", [2, 512, N]); sxs = din(nc, "sxs", [512, N]); sz = din(nc, "sz", [512, N])
    hyo = din(nc, "hyo", [256, N])
    pv = din(nc, "pv", [128, 10])
    w = din(nc, "w", [1024, 1024])
    mod = din(nc, "mod", [128, 6, 8])
    g2 = din(nc, "g2", [128, 8])
    bones = din(nc, "bones", [128, 128])
    x1o = dout(nc, "x1o", [1024, N]); h2o = dout(nc, "h2o", [1024, N])
    ones = load_consts(p, nc)
    psr = PsRot(p, 7)
    xs = p.sb('xs', [128, 8, N]); oT = p.sb('oT', [128, 8, N], BF16)
    tb = [p.sb('t%d' % i, [128, N]) for i in range(4)]
    pvs = p.sb('pvs', [128, 10]); mods = p.sb('mods', [128, 6, 8]); g2s = p.sb('g2s', [128, 8]); bo = p.sb('bo', [128, 128])
    gm = p.sb('gm', [128, 2, 8]); sh = p.sb('sh', [128, 2, 8])
    rstd = p.sb('rstd', [128, 512]); sq = [p.sb('sq%d' % i, [128, 512]) for i in range(2)]
    wst = [p.sb('wst%d' % i, [128, 8, 512]) for i in range(2)]
    wbf = [p.sb('wbf%d' % i, [128, 8, 512], BF16) for i in range(2)]
    hst = [p.sb('hst%d' % i, [128, 512]) for i in range(2)]
    for a, b in [(pvs, pv), (mods, mod), (g2s, g2), (bo, bones)]:
        p.dma(a, b)
    xv = xT.re("(k p) n -> p k n", p=128)
    for k in range(8):
        p.dma(xs[:, k, :], xv[:, k, :])
    for i in range(2):
        p.ts(gm[:, i, :], mods[:, 3 * i + 2, :], 1.0, ALU.add)
        p.tt(gm[:, i, :], gm[:, i, :], g2s, ALU.mult)
        p.copy(sh[:, i, :], mods[:, 3 * i + 1, :])
    for k in range(2):
        rs = slice(128 * k, 128 * (k + 1))
        p.dma(tb[0], hgo[0, rs, :]); p.dma(tb[1], hgo[1, rs, :]); p.dma(tb[2], hgg[rs, :])
        p.tt(tb[0], tb[0], tb[1], ALU.add)
        p.act(tb[2], tb[2], ACT.Silu)
        for (t0, tn) in tiles(N, 512):
            ps = psr.get()
            p.act(sq[0][:, :tn], tb[0][:, t0:t0 + tn], ACT.Square)
            p.mm(ps[:, :tn], bo, sq[0][:, :tn])
            p.act(rstd[:, :tn], ps[:, :tn], ACT.Sqrt, scale=1.0 / 64, bias=p.epsc)
            p.recip(rstd[:, :tn], rstd[:, :tn])
            p.tt(tb[1][:, t0:t0 + tn], tb[0][:, t0:t0 + tn], rstd[:, :tn], ALU.mult)
        p.stt(oT[:, k, :], tb[1], pvs[:, k:k + 1], tb[2], ALU.mult, ALU.mult)
    for k in range(2):
        p.dma(tb[0], hyo[128 * k:128 * (k + 1), :])
        p.copy(oT[:, 2 + k, :], tb[0], eng='pool')
    for g in range(2):
        ys = [tb[0], tb[1]]
        for c in range(2):
            k = 2 * g + c
            rs = slice(128 * k, 128 * (k + 1))
            p.dma(ys[c], sy[0, rs, :]); p.dma(tb[2], sy[1, rs, :])
            p.tt(ys[c], ys[c], tb[2], ALU.add)
            p.dma(tb[2], sxs[rs, :])
            p.stt(ys[c], tb[2], pvs[:, 2 + k:3 + k], ys[c], ALU.mult, ALU.add)
            p.dma(tb[3], sz[rs, :])
            p.act(tb[3], tb[3], ACT.Silu)
            p.tt(ys[c], ys[c], tb[3], ALU.mult)
        for (t0, tn) in tiles(N, 512):
            ps = psr.get()
            for c in range(2):
                p.act(sq[c][:, :tn], ys[c][:, t0:t0 + tn], ACT.Square)
                p.mm(ps[:, :tn], ones, sq[c][:, :tn], start=(c == 0), stop=(c == 1))
            p.act(rstd[:, :tn], ps[:, :tn], ACT.Sqrt, scale=1.0 / 256, bias=p.epsc)
            p.recip(rstd[:, :tn], rstd[:, :tn])
            for c in range(2):
                k = 2 * g + c
                p.stt(oT[:, 4 + k, t0:t0 + tn], ys[c][:, t0:t0 + tn], pvs[:, 6 + k:7 + k], rstd[:, :tn], ALU.mult, ALU.mult)

    def evac(m0, mn, t0, tn, ps):
        m = m0 // 128
        mi = 0 if t0 < NL else 1
        p.stt(xs[:, m, t0:t0 + tn], ps[:, :tn], mods[:, 3 * mi, m:m + 1], xs[:, m, t0:t0 + tn], ALU.mult, ALU.add)
    fm_linear(p, psr, w, 1024, 1024, oT, N, evac, wst, wbf)
    x1v = x1o.re("(k p) n -> p k n", p=128)
    for k in range(8):
        p.dma(x1v[:, k, :], xs[:, k, :])
    h2v = h2o.re("(k p) n -> p k n", p=128)
    segs = [(0, NL, 0)] + ([(NL, NCX, 1)] if NCX else [])
    cnt = 0
    for (s0, sn, mi) in segs:
        for (t0, tn) in [(s0 + a, b) for a, b in tiles(sn, 512)]:
            ps = psr.get()
            for k in range(8):
                p.act(sq[k % 2][:, :tn], xs[:, k, t0:t0 + tn], ACT.Square)
                p.mm(ps[:, :tn], ones, sq[k % 2][:, :tn], start=(k == 0), stop=(k == 7))
            p.act(rstd[:, :tn], ps[:, :tn], ACT.Sqrt, scale=1.0 / 1024, bias=p.epsc)
            p.recip(rstd[:, :tn], rstd[:, :tn])
            for k in range(8):
                hb = hst[cnt % 2]
                cnt += 1
                p.tt(hb[:, :tn], xs[:, k, t0:t0 + tn], rstd[:, :tn], ALU.mult)
                p.ts(hb[:, :tn], hb[:, :tn], gm[:, mi, k:k + 1], ALU.mult, sh[:, mi, k:k + 1], ALU.add)
                p.dma(h2v[:, k, t0:t0 + tn], hb[:, :tn])
    p.wait_all('sp')
    p.emit()
    return nc


def cat_lc(lat, ctx, b, q, has_ctx):
    a = lat[b][:, 2048 * q:2048 * (q + 1)]
    if has_ctx:
        a = np.concatenate([a, ctx[b][:, 64 * q:64 * (q + 1)]], 1)
    return np.ascontiguousarray(a)


def stage_C1(xT_lat, xT_ctx, uL, uC, oL, oC, yL, yC, xsL, xsC, ohL, ohC, modl, P):
    has_ctx = xT_ctx is not None
    NCX = 64 if has_ctx else 0
    nc = build_C1(2048, NCX)
    bones = np.kron(np.eye(2, dtype=np.float32), np.ones((64, 64), np.float32))
    pv = np.concatenate([P['hg_norm_g'].reshape(2, 128).T, np.repeat(P['ssd_d'], 64).reshape(4, 128).T,
                         P['ssd_norm_g'].reshape(4, 128).T], 1).astype(np.float32)
    maps = []
    base = 1280 + 768
    for i in range(8):
        b, q = i // 4, i % 4

        def cl(lat, ctx):
            return cat_lc(lat, ctx, b, q, has_ctx)

        def mv(j, col):
            return modvec(modl[1024 * j:1024 * (j + 1), col])
        mod = np.stack([mv(2, b), mv(3, b), mv(4, b), mv(2, 2), mv(3, 2), mv(4, 2)], 1)
        m = {"xT": cl(xT_lat, xT_ctx),
             "hgo": np.stack([cl(oL[0], oC[0] if has_ctx else None), cl(oL[1], oC[1] if has_ctx else None)]),
             "hgg": cl(uL[:, 1024:1280], uC[:, 1024:1280] if has_ctx else None),
             "sy": np.stack([cl(yL[0], yC[0] if has_ctx else None), cl(yL[1], yC[1] if has_ctx else None)]),
             "sxs": cl(xsL, xsC), "sz": cl(uL[:, base:base + 512], uC[:, base:base + 512] if has_ctx else None),
             "hyo": cl(ohL, ohC), "pv": np.ascontiguousarray(pv), "w": P['w_out'],
             "mod": np.ascontiguousarray(mod), "g2": modvec(P['norm2_g']), "bones": bones}
        maps.append(m)
    res = run(nc, maps)
    x1L = np.zeros((2, 1024, 8192), np.float32); h2L = np.zeros((2, 1024, 8192), np.float32)
    x1C = np.zeros((2, 1024, 256), np.float32) if has_ctx else None
    h2C = np.zeros((2, 1024, 256), np.float32) if has_ctx else None
    for i in range(8):
        b, q = i // 4, i % 4
        x1L[b][:, 2048 * q:2048 * (q + 1)] = res[i]["x1o"][:, :2048]
        h2L[b][:, 2048 * q:2048 * (q + 1)] = res[i]["h2o"][:, :2048]
        if has_ctx:
            x1C[b][:, 64 * q:64 * (q + 1)] = res[i]["x1o"][:, 2048:]
            h2C[b][:, 64 * q:64 * (q + 1)] = res[i]["h2o"][:, 2048:]
    return x1L, h2L, x1C, h2C


def build_C2(Wd, R):
    NH = (R + 2) * Wd
    NI = R * Wd
    nc = bass.Bass("TRN2", target_bir_lowering=False)
    p = Prog(nc)
    h2h = din(nc, "h2h", [1024, NH]); x1 = din(nc, "x1", [1024, NI])
    wg = din(nc, "wg", [1024, 2816]); wu = din(nc, "wu", [1024, 2816]); wd = din(nc, "wd", [2816, 1024])
    cwt = din(nc, "cwt", [128, 22, 9]); cbt = din(nc, "cbt", [128, 22]); gate2 = din(nc, "gate2", [128, 8])
    x2o = dout(nc, "x2o", [1024, NI])
    psr = PsRot(p, 7)
    h2b = p.sb('h2b', [128, 8, NH], BF16)
    act = p.sb('act', [128, 22, NI], BF16)
    a_s = p.sb('a_s', [128, NH]); acc = p.sb('acc', [128, NI])
    cws = p.sb('cws', [128, 22, 9]); cbs = p.sb('cbs', [128, 22]); g2s = p.sb('g2s', [128, 8])
    wst = [p.sb('wst%d' % i, [128, 8, 256]) for i in range(2)]
    wbg = [p.sb('wbg%d' % i, [128, 8, 256], BF16) for i in range(2)]
    wbu = [p.sb('wbu%d' % i, [128, 8, 256], BF16) for i in range(2)]
    wdst = [p.sb('wdst0', [128, 22, 128])] * 2
    wdbf = [p.sb('wdbf%d' % i, [128, 22, 128], BF16) for i in range(2)]
    xt = [p.sb('xt%d' % i, [128, 512]) for i in range(3)]
    p.dma(cws, cwt); p.dma(cbs, cbt); p.dma(g2s, gate2)
    hv = h2h.re("(k p) n -> p k n", p=128)
    for k in range(8):
        p.dma(a_s, hv[:, k, :])
        p.copy(h2b[:, k, :], a_s, eng='pool' if k % 2 else 'dve')
    wgv = wg.re("(k p) n -> p k n", p=128); wuv = wu.re("(k p) n -> p k n", p=128)
    for bi, (c0, cn) in enumerate(tiles(2816, 256)):
        p.dma(wst[0][:, :, :cn], wgv[:, :, c0:c0 + cn])
        p.copy(wbg[bi % 2][:, :, :cn], wst[0][:, :, :cn], eng='pool')
        p.dma(wst[1][:, :, :cn], wuv[:, :, c0:c0 + cn])
        p.copy(wbu[bi % 2][:, :, :cn], wst[1][:, :, :cn], eng='pool')
        for fi in range(cn // 128):
            f = c0 // 128 + fi
            fs = slice(128 * fi, 128 * (fi + 1))
            for (t0, tn) in tiles(NH, 512):
                ps = psr.get()
                for k in range(8):
                    p.mm(ps[:, :tn], wbg[bi % 2][:, k, fs], h2b[:, k, t0:t0 + tn], start=(k == 0), stop=(k == 7))
                p.copy(a_s[:, t0:t0 + tn], ps[:, :tn], eng='act')
            p.ts(acc, a_s[:, Wd:Wd + NI], cws[:, f, 4:5], ALU.mult, cbs[:, f:f + 1], ALU.add)
            for dr in (-1, 0, 1):
                for dw in (-1, 0, 1):
                    if dr == 0 and dw == 0:
                        continue
                    w0 = max(0, -dw); w1 = Wd - max(0, dw); cw_ = w1 - w0
                    tap = (dr + 1) * 3 + (dw + 1)
                    ov = acc.view(w0, [[Wd, R], [1, cw_]])
                    sv = a_s.view((1 + dr) * Wd + w0 + dw, [[Wd, R], [1, cw_]])
                    p.stt(ov, sv, cws[:, f, tap:tap + 1], ov, ALU.mult, ALU.add)
            p.act(acc, acc, ACT.Silu)
            for (t0, tn) in tiles(NI, 512):
                ps = psr.get()
                for k in range(8):
                    p.mm(ps[:, :tn], wbu[bi % 2][:, k, fs], h2b[:, k, Wd + t0:Wd + t0 + tn], start=(k == 0), stop=(k == 7))
                p.tt(act[:, f, t0:t0 + tn], ps[:, :tn], acc[:, t0:t0 + tn], ALU.mult)
    x1v = x1.re("(k p) n -> p k n", p=128); x2v = x2o.re("(k p) n -> p k n", p=128)
    st = {'n': 0}

    def evac(m0, mn, t0, tn, ps):
        m = m0 // 128
        i = st['n'] % 3
        st['n'] += 1
        p.dma(xt[i][:, :tn], x1v[:, m, t0:t0 + tn])
        p.stt(xt[i][:, :tn], ps[:, :tn], g2s[:, m:m + 1], xt[i][:, :tn], ALU.mult, ALU.add)
        p.dma(x2v[:, m, t0:t0 + tn], xt[i][:, :tn])
    fm_linear(p, psr, wd, 2816, 1024, act, NI, evac, wdst, wdbf, colblk=128)
    p.wait_all('sp')
    p.emit()
    return nc


def stage_C2(h2L, x1L, modl, P, ctx=False):
    cwt = np.ascontiguousarray(P['ffn_conv_w'].reshape(9, 22, 128).transpose(2, 1, 0))
    cbt = np.ascontiguousarray(P['ffn_conv_b'].reshape(22, 128).T)
    maps = []
    if not ctx:
        nc = build_C2(64, 32)
        for i in range(8):
            b, q = i // 4, i % 4
            hp = np.zeros((1024, 34 * 64), np.float32)
            lo = 2048 * q - 64; hi = 2048 * (q + 1) + 64
            s0 = max(lo, 0); s1 = min(hi, 8192)
            hp[:, s0 - lo:s1 - lo] = h2L[b][:, s0:s1]
            maps.append({"h2h": hp, "x1": np.ascontiguousarray(x1L[b][:, 2048 * q:2048 * (q + 1)]), "wg": P['ffn_w_gate'],
                         "wu": P['ffn_w_up'], "wd": P['ffn_w_down'], "cwt": cwt, "cbt": cbt,
                         "gate2": modvec(modl[5120:6144, b])})
        res = run(nc, maps)
        x2 = np.zeros((2, 1024, 8192), np.float32)
        for i in range(8):
            b, q = i // 4, i % 4
            x2[b][:, 2048 * q:2048 * (q + 1)] = res[i]["x2o"]
        return x2
    nc = build_C2(256, 1)
    for b in range(2):
        hp = np.zeros((1024, 768), np.float32)
        hp[:, 256:512] = h2L[b]
        maps.append({"h2h": hp, "x1": np.ascontiguousarray(x1L[b]), "wg": P['ffn_w_gate'], "wu": P['ffn_w_up'],
                     "wd": P['ffn_w_down'], "cwt": cwt, "cbt": cbt, "gate2": modvec(modl[5120:6144, 2])})
    res = run(nc, maps, ncores=2)
    return np.stack([res[0]["x2o"], res[1]["x2o"]])


def build_F(N):
    nc = bass.Bass("TRN2", target_bir_lowering=False)
    p = Prog(nc)
    xT = din(nc, "xT", [1024, N]); g = din(nc, "g", [128, 8]); o = dout(nc, "o", [1024, N])
    ones = load_consts(p, nc)
    psr = PsRot(p, 4)
    xs = p.sb('xs', [128, 8, N]); ho = p.sb('ho', [128, 8, N])
    gm = p.sb('gm', [128, 1, 8]); sh = p.sb('sh', [128, 1, 8])
    rstd = p.sb('rstd', [128, 512])
    sqb = [p.sb('sq%d' % i, [128, 512]) for i in range(2)]
    tmpb = [p.sb('tb%d' % i, [128, 512]) for i in range(2)]
    p.dma(gm[:, 0, :], g)
    p.memset(sh, 0.0)
    xv = xT.re("(k p) n -> p k n", p=128); ov = o.re("(k p) n -> p k n", p=128)
    for k in range(8):
        p.dma(xs[:, k, :], xv[:, k, :])
    fm_norm_mod(p, psr, xs, 8, [(0, N, 0)], ones, gm, sh, ho, rstd, sqb, tmpb, 1024.0)
    for k in range(8):
        p.dma(ov[:, k, :], ho[:, k, :])
    p.wait_all('sp')
    p.emit()
    return nc


def stage_F(xT_lat, g):
    nc = build_F(2048)
    maps = []
    for i in range(8):
        b, q = i // 4, i % 4
        maps.append({"xT": np.ascontiguousarray(xT_lat[b][:, 2048 * q:2048 * (q + 1)]), "g": modvec(g)})
    res = run(nc, maps)
    out = np.zeros((2, 8192, 1024), np.float32)
    for i in range(8):
        b, q = i // 4, i % 4
        out[b, 2048 * q:2048 * (q + 1), :] = res[i]["o"].T
    return out


LAYER_KEYS = ["w_in", "w_out", "hg_norm_g", "hy_conv_w", "hy_conv_b", "hy_w1", "hy_b1", "hy_freq1", "hy_w2", "hy_b2",
              "hy_freq2", "hy_w3", "hy_bias", "ssd_conv_w", "ssd_conv_b", "ssd_dt_bias", "ssd_a_log", "ssd_d",
              "ssd_norm_g", "ffn_w_gate", "ffn_w_up", "ffn_conv_w", "ffn_conv_b", "ffn_w_down", "norm1_g", "norm2_g"]


def kernel(**inp):
    inp = {k: np.asarray(v, dtype=np.float32) for k, v in inp.items()}
    mods = stage_M(inp['c'], inp['c_ctx'], inp['w_ada'], inp['b_ada'])
    xT = np.ascontiguousarray(inp['x'].transpose(0, 2, 1))
    cT = np.ascontiguousarray(inp['ctx'].transpose(0, 2, 1))
    for l in range(2):
        P = {k: np.ascontiguousarray(inp[k][l]) for k in LAYER_KEYS}
        last = l == 1
        uL, uC = stage_A(xT, cT, mods[l], P['norm1_g'], P['w_in'])
        oL, oC = stage_B1(uL, uC, inp['hg_lb_logits'], float(l))
        yL, yC, xsL, xsC = stage_B2(uL, uC, P['ssd_conv_w'], P['ssd_conv_b'], P['ssd_dt_bias'], P['ssd_a_log'])
        ohL = stage_B3(np.ascontiguousarray(uL[:, 1280:2048, :]), 8192, P)
        if not last:
            ohC = stage_B3(np.ascontiguousarray(uC[:, 1280:2048, :]), 256, P)
            x1L, h2L, x1C, h2C = stage_C1(xT, cT, uL, uC, oL, oC, yL, yC, xsL, xsC, ohL, ohC, mods[l], P)
            cT = stage_C2(h2C, x1C, mods[l], P, ctx=True)
        else:
            x1L, h2L, _, _ = stage_C1(xT, None, uL, None, oL, None, yL, None, xsL, None, ohL, None, mods[l], P)
        xT = stage_C2(h2L, x1L, mods[l], P)
    return stage_F(xT, inp['final_norm_g'])
```
